# Optimizing a Trainium2 kernel written in Bass

```python
import jax, jax.numpy as jnp
from jax import lax
import numpy as np

D_MODEL = 1024
BATCH = 16
SEQ = 4096
DEPTH = 1
DEC_BATCH = 8
DEC_SEQ = 16
PAST_LEN = 4096

CHUNK = 64
N_ATT_HEADS = 8
HEAD_DIM = 64
ATT_DIM = N_ATT_HEADS * HEAD_DIM
N_CONV_GROUPS = 8
CONV_DIM = N_CONV_GROUPS * HEAD_DIM
CONV_WIDTH = 3
D_FF = 4 * D_MODEL
Q_BLOCK = 128
NORM_EPS = 1e-6
PROJ_DIM = 3 * ATT_DIM + N_ATT_HEADS + 3 * CONV_DIM
ATT_SCALE = HEAD_DIM ** -0.5

kernel_name = 'hymba_fox_shortconv_streaming_step'


def rmsnorm(x, g):
    xf = x.astype(jnp.float32)
    y = xf * lax.rsqrt(jnp.mean(xf * xf, axis=-1, keepdims=True) + NORM_EPS)
    return (y * g.astype(jnp.float32)).astype(x.dtype)


def modulate(h, shift, scale):
    return h * (1 + scale[:, None, :]) + shift[:, None, :]


def to_heads(t):
    return t.reshape(t.shape[:-1] + (N_ATT_HEADS, HEAD_DIM))


def mix_inputs(h, w_in, b_f):
    p = h @ w_in
    cuts = [ATT_DIM, 2 * ATT_DIM, 3 * ATT_DIM, 3 * ATT_DIM + N_ATT_HEADS,
            3 * ATT_DIM + N_ATT_HEADS + CONV_DIM, 3 * ATT_DIM + N_ATT_HEADS + 2 * CONV_DIM]
    q, k, v, fl, bg, cg, u = jnp.split(p, cuts, axis=-1)
    logf = jax.nn.log_sigmoid(fl.astype(jnp.float32) + b_f.astype(jnp.float32))
    return to_heads(q), to_heads(k), to_heads(v), logf, bg, cg * u


def fox_prompt(q, k, v, logf):
    B, S = q.shape[0], q.shape[1]
    nb = S // Q_BLOCK
    F = jnp.cumsum(logf, axis=1)
    Fk = F.transpose(0, 2, 1)[:, :, None, :]
    qb = q.reshape(B, nb, Q_BLOCK, N_ATT_HEADS, HEAD_DIM).transpose(1, 0, 2, 3, 4)
    Fqb = F.reshape(B, nb, Q_BLOCK, N_ATT_HEADS).transpose(1, 0, 3, 2)
    kpos = jnp.arange(S)

    def block(args):
        qi, fqi, bi = args
        s = jnp.einsum('bqhd,bkhd->bhqk', qi, k, preferred_element_type=jnp.float32) * ATT_SCALE
        s = s + fqi[..., None] - Fk
        qpos = bi * Q_BLOCK + jnp.arange(Q_BLOCK)
        s = jnp.where(kpos[None, :] <= qpos[:, None], s, -jnp.inf)
        p = jax.nn.softmax(s, axis=-1)
        return jnp.einsum('bhqk,bkhd->bqhd', p.astype(v.dtype), v)

    o = lax.map(block, (qb, Fqb, jnp.arange(nb)))
    return o.transpose(1, 0, 2, 3, 4).reshape(B, S, ATT_DIM)


def fox_sample(q, k, v, logf, ck, cv, clogf):
    B, T = q.shape[0], q.shape[1]
    P = ck.shape[1]
    k_all = jnp.concatenate([ck.astype(k.dtype), k], axis=1)
    v_all = jnp.concatenate([cv.astype(v.dtype), v], axis=1)
    F = jnp.cumsum(jnp.concatenate([clogf.astype(jnp.float32), logf], axis=1), axis=1)
    Fq = F[:, P:].transpose(0, 2, 1)[..., None]
    Fk = F.transpose(0, 2, 1)[:, :, None, :]
    s = jnp.einsum('bqhd,bkhd->bhqk', q, k_all, preferred_element_type=jnp.float32) * ATT_SCALE
    s = s + Fq - Fk
    mask = jnp.arange(P + T)[None, :] <= (P + jnp.arange(T))[:, None]
    s = jnp.where(mask, s, -jnp.inf)
    p = jax.nn.softmax(s, axis=-1)
    o = jnp.einsum('bhqk,bkhd->bqhd', p.astype(v_all.dtype), v_all)
    return o.reshape(B, T, ATT_DIM)


def short_conv(u, prev, w_conv):
    ue = jnp.concatenate([prev.astype(u.dtype), u], axis=1)
    y = lax.conv_general_dilated(ue, w_conv[:, None, :].astype(u.dtype), window_strides=(1,),
                                 padding='VALID', dimension_numbers=('NWC', 'WIO', 'NWC'),
                                 feature_group_count=CONV_DIM)
    return y, ue[:, -(CONV_WIDTH - 1):]


def run_layer(x, c, w_ada, b_ada, g1, g2, w_in, b_f, w_conv, g_att, g_conv, w_out, w_up, w_down,
              attend, conv_prev):
    sh1, sc1, gt1, sh2, sc2, gt2 = jnp.split(jax.nn.silu(c) @ w_ada + b_ada, 6, axis=-1)
    h = modulate(rmsnorm(x, g1), sh1, sc1)
    q, k, v, logf, bg, u = mix_inputs(h, w_in, b_f)
    att = attend(q, k, v, logf)
    cv, conv_state = short_conv(u, conv_prev, w_conv)
    merged = jnp.concatenate([rmsnorm(att, g_att), rmsnorm(bg * cv, g_conv)], axis=-1)
    x = x + gt1[:, None, :] * (merged @ w_out)
    h = modulate(rmsnorm(x, g2), sh2, sc2)
    x = x + gt2[:, None, :] * (jnp.square(jax.nn.relu(h @ w_up)) @ w_down)
    return x, k, v, logf, conv_state


def setup_inputs(seed: int = 0) -> dict:
    key = jax.random.key(seed)
    ks = jax.random.split(key, 24)

    def nrm(k, shape, scale=1.0):
        return jax.random.normal(k, shape, jnp.float32) * scale

    return {
        'x_prompt': nrm(ks[0], (BATCH, SEQ, D_MODEL)),
        'x_sample': nrm(ks[1], (DEC_BATCH, DEC_SEQ, D_MODEL)),
        'cache_k': nrm(ks[2], (DEPTH, DEC_BATCH, PAST_LEN, N_ATT_HEADS, HEAD_DIM)),
        'cache_v': nrm(ks[3], (DEPTH, DEC_BATCH, PAST_LEN, N_ATT_HEADS, HEAD_DIM)),
        'cache_logf': jax.nn.log_sigmoid(3.0 + nrm(ks[4], (DEPTH, DEC_BATCH, PAST_LEN, N_ATT_HEADS))),
        'cache_conv': nrm(ks[5], (DEPTH, DEC_BATCH, CONV_WIDTH - 1, CONV_DIM), 0.5),
        'c_prompt': nrm(ks[6], (BATCH, D_MODEL)),
        'c_sample': nrm(ks[7], (DEC_BATCH, D_MODEL)),
        'w_ada': nrm(ks[8], (DEPTH, D_MODEL, 6 * D_MODEL), 0.2 * D_MODEL ** -0.5),
        'b_ada': nrm(ks[9], (DEPTH, 6 * D_MODEL), 0.01),
        'g_norm1': 1.0 + nrm(ks[10], (DEPTH, D_MODEL), 0.05),
        'g_norm2': 1.0 + nrm(ks[11], (DEPTH, D_MODEL), 0.05),
        'w_in': nrm(ks[12], (DEPTH, D_MODEL, PROJ_DIM), D_MODEL ** -0.5),
        'b_f': 3.0 + nrm(ks[13], (DEPTH, N_ATT_HEADS), 0.5),
        'w_conv': nrm(ks[14], (DEPTH, CONV_WIDTH, CONV_DIM), CONV_WIDTH ** -0.5),
        'g_attn_out': 1.0 + nrm(ks[15], (DEPTH, ATT_DIM), 0.05),
        'g_conv_out': 1.0 + nrm(ks[16], (DEPTH, CONV_DIM), 0.05),
        'w_out': nrm(ks[17], (DEPTH, D_MODEL, D_MODEL), D_MODEL ** -0.5),
        'w_up': nrm(ks[18], (DEPTH, D_MODEL, D_FF), D_MODEL ** -0.5),
        'w_down': nrm(ks[19], (DEPTH, D_FF, D_MODEL), D_FF ** -0.5),
        'w_ada_final': nrm(ks[20], (D_MODEL, 2 * D_MODEL), 0.2 * D_MODEL ** -0.5),
        'b_ada_final': nrm(ks[21], (2 * D_MODEL,), 0.01),
        'g_final': 1.0 + nrm(ks[22], (D_MODEL,), 0.05),
    }


def reference(x_prompt, x_sample, cache_k, cache_v, cache_logf, cache_conv, c_prompt, c_sample,
              w_ada, b_ada, g_norm1, g_norm2, w_in, b_f, w_conv, g_attn_out, g_conv_out, w_out,
              w_up, w_down, w_ada_final, b_ada_final, g_final):
    xp, xs = x_prompt, x_sample
    kp, vp, lp, cp = [], [], [], []
    ksl, vsl, lsl, csl = [], [], [], []
    for l in range(DEPTH):
        lw = (w_ada[l], b_ada[l], g_norm1[l], g_norm2[l], w_in[l], b_f[l], w_conv[l],
              g_attn_out[l], g_conv_out[l], w_out[l], w_up[l], w_down[l])
        prev0 = jnp.zeros((xp.shape[0], CONV_WIDTH - 1, CONV_DIM), xp.dtype)
        xp, k1, v1, f1, s1 = run_layer(xp, c_prompt, *lw, fox_prompt, prev0)
        ck, cv, cf = cache_k[l], cache_v[l], cache_logf[l]
        attend_s = lambda q, k, v, f, ck=ck, cv=cv, cf=cf: fox_sample(q, k, v, f, ck, cv, cf)
        xs, k2, v2, f2, s2 = run_layer(xs, c_sample, *lw, attend_s, cache_conv[l])
        kp.append(k1); vp.append(v1); lp.append(f1); cp.append(s1)
        ksl.append(k2); vsl.append(v2); lsl.append(f2); csl.append(s2)

    def final(x, c):
        sh, sc = jnp.split(jax.nn.silu(c) @ w_ada_final + b_ada_final, 2, axis=-1)
        return modulate(rmsnorm(x, g_final), sh, sc)

    y_prompt = final(xp, c_prompt)
    y_sample = final(xs, c_sample)
    return (y_prompt, y_sample,
            jnp.stack(kp), jnp.stack(vp), jnp.stack(lp), jnp.stack(cp),
            jnp.stack(ksl), jnp.stack(vsl), jnp.stack(lsl), jnp.stack(csl))
```

```python
import numpy as np
import concourse.bass as bass
import concourse.mybir as mybir
from concourse.bass_utils import run_bass_kernel_spmd

F32 = mybir.dt.float32
BF16 = mybir.dt.bfloat16
AF = mybir.ActivationFunctionType
ALU = mybir.AluOpType

D = 1024
NH = 8
HD = 64
ATT = 512
CONV = 512
DFF = 4096
PROJ = 3080
EPS = 1e-6
ATT_SCALE = HD ** -0.5
NCORES = 8

ENGS = ("pe", "act", "dve", "pool", "sp")


class Reg:
    __slots__ = ("ap", "arena", "lo", "hi")

    def __init__(self, ap, arena, lo, hi):
        self.ap, self.arena, self.lo, self.hi = ap, arena, lo, hi


class Buf:
    def __init__(self, ap, arena, base, shape, esize):
        self.ap, self.arena, self.base, self.shape, self.esize = ap, arena, base, tuple(shape), esize
        st = []
        s = 1
        for d in reversed(self.shape[1:]):
            st.append(s)
            s *= d
        self.strides = tuple(reversed(st))
        self.row = s

    def __getitem__(self, key):
        if not isinstance(key, tuple):
            key = (key,)
        key = key + (slice(None),) * (len(self.shape) - len(key))
        lo = 0
        hi = 0
        for k, d, st in zip(key[1:], self.shape[1:], self.strides):
            if isinstance(k, slice):
                a = 0 if k.start is None else k.start
                b = d if k.stop is None else k.stop
                assert 0 <= a < b <= d, (key, self.shape)
            else:
                a, b = k, k + 1
                assert 0 <= a < d, (key, self.shape)
            lo += a * st
            hi += (b - 1) * st
        hi += 1
        return Reg(self.ap[key], self.arena, self.base + lo * self.esize, self.base + hi * self.esize)

    def all(self):
        return self[tuple(slice(None) for _ in self.shape)]


def esize_of(dt):
    return 2 if dt == BF16 else 4


class Sched:
    def __init__(self, nc):
        self.nc = nc
        self.ops = []
        self.cells = {}
        self.gran = {}
        self.stack = None

    def cellrange(self, r):
        g = self.gran.get(r.arena)
        if g is None:
            g = 2048 if r.arena == "psum" else (1 if r.arena.startswith("dram") else 128)
            self.gran[r.arena] = g
        return range(r.lo // g, (r.hi - 1) // g + 1)

    def add(self, eng, fn, reads=(), writes=(), dma=False):
        i = len(self.ops)
        deps = set()
        writes = list(writes) + [r for r in reads if r.arena == "psum"]
        for r in reads:
            for c in self.cellrange(r):
                st = self.cells.get((r.arena, c))
                if st is None:
                    st = [None, {}, []]
                    self.cells[(r.arena, c)] = st
                if st[0] is not None:
                    deps.add(st[0])
        for r in writes:
            for c in self.cellrange(r):
                st = self.cells.get((r.arena, c))
                if st is None:
                    st = [None, {}, []]
                    self.cells[(r.arena, c)] = st
                if st[0] is not None:
                    deps.add(st[0])
                deps.update(st[1].values())
                deps.update(st[2])
        for r in reads:
            for c in self.cellrange(r):
                st = self.cells[(r.arena, c)]
                if dma:
                    st[2].append(i)
                else:
                    st[1][eng] = i
        for r in writes:
            for c in self.cellrange(r):
                st = self.cells[(r.arena, c)]
                st[0] = i
                st[1] = {}
                st[2] = []
        deps.discard(i)
        self.ops.append(dict(eng=eng, fn=fn, dma=dma, deps=deps,
                             dbg=([(r.arena, r.lo, r.hi) for r in reads], [(r.arena, r.lo, r.hi) for r in writes])))
        return i

    def emit(self, n_dma_sems=24):
        nc = self.nc
        ops = self.ops
        for o in ops:
            o["sig"] = False
        for o in ops:
            need = []
            for d in o["deps"]:
                p = ops[d]
                if p["dma"] or p["eng"] != o["eng"] or o["eng"] != "pe":
                    need.append(d)
                    if not p["dma"]:
                        p["sig"] = True
            o["need"] = need
        tick = {e: 0 for e in ENGS}
        for o in ops:
            if o["dma"]:
                continue
            if o["sig"]:
                tick[o["eng"]] += 1
                o["tick"] = tick[o["eng"]]
        dma_cnt = [0] * n_dma_sems
        rng = {"pool": (0, 3), "sp": (8, 16), "act": (16, n_dma_sems)}
        rr = {"pool": 0, "sp": 0, "act": 0}
        for o in ops:
            if o["dma"]:
                a, b_ = rng[o["eng"]]
                s = a + rr[o["eng"]] % (b_ - a)
                rr[o["eng"]] += 1
                o["dsem"] = s
                o["dprev"] = dma_cnt[s]
                dma_cnt[s] += 16
                o["dval"] = dma_cnt[s]
        seen = {e: {x: 0 for x in ENGS} for e in ENGS}
        seen_d = {e: [0] * n_dma_sems for e in ENGS}
        per_eng = {e: [] for e in ENGS}
        for o in ops:
            e = o["eng"]
            waits = []
            best = {}
            for d in o["need"]:
                p = ops[d]
                if p["dma"]:
                    if seen_d[e][p["dsem"]] < p["dval"]:
                        seen_d[e][p["dsem"]] = p["dval"]
                        waits.append(("d", p["dsem"], p["dval"]))
                else:
                    if best.get(p["eng"], (0, None))[0] < p["tick"]:
                        best[p["eng"]] = (p["tick"], p)
            for pe_, (t, p) in best.items():
                if seen[e][pe_] < t:
                    waits.append(("c", pe_, t))
                    seen[e][pe_] = t
                    for x, v in p["snap"].items():
                        if seen[e][x] < v:
                            seen[e][x] = v
            if o["dma"]:
                if seen_d[e][o["dsem"]] < o["dprev"]:
                    seen_d[e][o["dsem"]] = o["dprev"]
                    waits.append(("d", o["dsem"], o["dprev"]))
            o["waits"] = waits
            if (not o["dma"]) and o["sig"]:
                o["snap"] = dict(seen[e])
                o["snap"][e] = o["tick"]
            per_eng[e].append(o)
        self.stats = {e: len(per_eng[e]) for e in ENGS}
        self.stats["ticks"] = dict(tick)
        final_dma = list(dma_cnt)

        import contextlib

        with contextlib.ExitStack() as st:
            csem = {e: st.enter_context(nc.semaphore("c_" + e)) for e in ENGS if e != "sp"}
            dsem = [st.enter_context(nc.semaphore("d%d" % i)) for i in range(n_dma_sems)]
            block = st.enter_context(nc.Block())

            def run(engname, e):
                for o in per_eng[engname]:
                    for w in o["waits"]:
                        if w[0] == "c":
                            e.wait_ge(csem[w[1]], w[2])
                        else:
                            e.wait_ge(dsem[w[1]], w[2])
                    ins = o["fn"](e)
                    if o["dma"]:
                        ins.then_inc(dsem[o["dsem"]], 16)
                    elif o["sig"]:
                        ins.then_inc(csem[engname], 1)
                if engname == "sp":
                    for i, v in enumerate(final_dma):
                        if v:
                            e.wait_ge(dsem[i], v)

            @block.tensor
            def _(e):
                run("pe", e)

            @block.scalar
            def _(e):
                run("act", e)

            @block.vector
            def _(e):
                run("dve", e)

            @block.gpsimd
            def _(e):
                run("pool", e)

            @block.sync
            def _(e):
                run("sp", e)


def build_program(cfg):
    S = cfg["S"]
    NSEQ = cfg["NSEQ"]
    PAST = cfg["P"]
    T = cfg["T"]
    SAMPLE = cfg.get("SAMPLE", True)
    GW = 512
    nc = bass.Bass("TRN2", target_bir_lowering=False)
    sc = Sched(nc)
    add = sc.add

    def din(name, shape, dt=F32):
        return nc.dram_tensor(name, list(shape), dt, kind="ExternalInput").ap()

    def dout(name, shape, dt=F32):
        return nc.dram_tensor(name, list(shape), dt, kind="ExternalOutput").ap()

    def dscr(name, shape, dt):
        return nc.dram_tensor(name, list(shape), dt, kind="Internal").ap()

    xp = din("xp", [NSEQ, S, D])
    cvec = din("cvec", [4, D])
    xs = din("xs", [T, D])
    ck = din("ck", [PAST, ATT])
    cv = din("cv", [PAST, ATT])
    clf = din("clf", [PAST, NH])
    cconv = din("cconv", [2, CONV])
    w_ada = din("w_ada", [D, 6 * D])
    b_ada = din("b_ada", [6 * D])
    g1 = din("g1", [D])
    g2 = din("g2", [D])
    w_in = din("w_in", [D, PROJ])
    b_f = din("b_f", [NH])
    w_conv = din("w_conv", [3, CONV])
    g_att = din("g_att", [ATT])
    g_conv = din("g_conv", [CONV])
    w_out = din("w_out", [D, D])
    w_up = din("w_up", [D, DFF])
    w_down = din("w_down", [DFF, D])
    w_adaf = din("w_adaf", [D, 2 * D])
    b_adaf = din("b_adaf", [2 * D])
    g_f = din("g_f", [D])

    yp = dout("yp", [NSEQ, S, D])
    kp = dout("kp", [NSEQ, S, ATT])
    vp = dout("vp", [NSEQ, S, ATT])
    lfp = dout("lfp", [NSEQ, S, NH])
    convp = dout("convp", [NSEQ, 2, CONV])
    ys = dout("ys", [T, D])
    ks = dout("ks", [T, ATT])
    vs = dout("vs", [T, ATT])
    lfs = dout("lfs", [T, NH])
    convs = dout("convs", [2, CONV])

    win_b = dscr("win_b", [D, PROJ], BF16)
    wout_b = dscr("wout_b", [D, D], BF16)
    wup_b = dscr("wup_b", [D, DFF], BF16)
    wdn_b = dscr("wdn_b", [DFF, D], BF16)

    import contextlib
    stack = contextlib.ExitStack()
    stack.__enter__()

    def sbuf(name, shape, dt):
        t = stack.enter_context(nc.sbuf_tensor(name, list(shape), dt))
        return Buf(t[tuple(slice(None) for _ in shape)], name, 0, shape, esize_of(dt))

    def view(parent_t, arena, off_bytes, shape, dt):
        es = esize_of(dt)
        n = 1
        for d in shape[1:]:
            n *= d
        nbytes = n * es
        assert off_bytes % 4 == 0 and nbytes % 4 == 0
        ap = parent_t[:, off_bytes // 4:(off_bytes + nbytes) // 4]
        if dt != F32:
            ap = ap.bitcast(dt)
        if len(shape) == 3:
            ap = ap.rearrange("p (a b) -> p a b", b=shape[2])
        elif len(shape) == 4:
            ap = ap.rearrange("p (a b c) -> p a b c", b=shape[2], c=shape[3])
        return Buf(ap, arena, off_bytes, shape, es)

    maxtiles = max(S // 128, (PAST // 128 + 1) if SAMPLE else 0)

    kT = sbuf("kT", [128, 4, maxtiles * 128], BF16)
    Vp = sbuf("Vp", [128, maxtiles, NH, 66], BF16)
    Vp2 = Buf(Vp.ap.rearrange("p t (h q) d -> p t h q d", q=2), "Vp", 0, [128, maxtiles, 4, 2, 66], 2)
    Ftok = sbuf("Ftok", [128, maxtiles, NH], F32)
    Eend = sbuf("Eend", [128, maxtiles + 1, NH], F32)
    kfac = sbuf("kfac", [128, maxtiles, NH], F32)
    xT = sbuf("xT", [128, 8, GW], F32)
    hT = sbuf("hT", [128, 8, GW], BF16)
    rstd = sbuf("rstd", [128, GW], F32)
    tmpA = sbuf("tmpA", [128, 2, GW], F32)
    sq = sbuf("sq", [128, 2, GW], BF16)
    wring = sbuf("wring", [128, 3, 8, 512], BF16)
    yst = sbuf("yst", [128, 2, D], F32)
    att0 = Buf(yst.ap.rearrange("p a (b c) -> p (a b) c", c=ATT), "yst", 0, [128, 4, ATT], 4)
    attn = sbuf("attn", [128, 4, ATT], BF16)
    vsraw = stack.enter_context(nc.sbuf_tensor("vsraw", [128, 520], F32))
    Vs = view(vsraw, "vsraw", 0, [128, 4, 4, 65], BF16)
    rbuf = view(vsraw, "vsraw", 0, [128, 2, GW], BF16)
    qpad = sbuf("qpad", [128, NH, GW], BF16)
    expD = sbuf("expD", [128, 4, maxtiles, NH], F32)
    small = sbuf("small", [128, 64], F32)
    lfst = sbuf("lfst", [128, 4, NH], F32)
    flx = sbuf("flx", [128, 4, NH], F32)
    identF = sbuf("identF", [128, 128], F32)
    identB = sbuf("identB", [128, 128], BF16)
    triF = sbuf("triF", [128, 128], F32)
    triB = sbuf("triB", [128, 128], BF16)
    selF = sbuf("selF", [128, 128], F32)
    onesF = sbuf("onesF", [128, 128], F32)
    onesB = sbuf("onesB", [128, 128], BF16)
    modc = sbuf("modc", [128, 3, 8, 8], F32)
    gcol = sbuf("gcol", [128, 5, 8], F32)
    wflb = sbuf("wflb", [128, 8, NH], BF16)
    wcv = sbuf("wcv", [128, 4, 3], F32)
    bfb = sbuf("bfb", [128, NH], F32)
    U1W = 33 * 1024
    u1 = stack.enter_context(nc.sbuf_tensor("u1", [128, U1W // 4], F32))
    aT = view(u1, "u1", 0, [128, 32, GW], BF16)
    modr = Buf(u1[0:4, 0:8 * D], "u1", 0, [4, 8 * D], 4)
    kst = view(u1, "u1", 0, [128, 4, ATT], F32)
    zT = view(u1, "u1", 0, [128, 4, GW], F32)
    vst = view(u1, "u1", 8192, [128, 4, ATT], F32)
    usb = view(u1, "u1", 16384, [128, GW], F32)
    cu = view(u1, "u1", 18432, [128, 4, GW + 2], F32)
    ycv = view(u1, "u1", 26656, [128, GW], F32)
    chist = sbuf("chist", [128, 4, 2], F32)
    u2 = stack.enter_context(nc.sbuf_tensor("u2", [128, 4 * D], F32))
    xin = view(u2, "u2", 0, [128, 4, D], F32)
    mT = view(u2, "u2", 0, [128, 8, GW], BF16)
    qT = view(u2, "u2", 8192, [128, 4, GW], BF16)
    Pr = view(u2, "u2", 12288, [128, 4, 512], BF16)

    psum_t = [stack.enter_context(nc.psum_tensor("ps%d" % i, [128, 512], F32)) for i in range(8)]

    def PS(bank, c0=0, c1=512, p0=0, p1=128):
        return Reg(psum_t[bank][p0:p1, c0:c1], "psum", bank * 2048 + c0 * 4, bank * 2048 + c1 * 4)

    def PSB(bank, c0, c1, p0=0, p1=128):
        ap = psum_t[bank][p0:p1, :].bitcast(BF16)[:, c0:c1]
        return Reg(ap, "psum", bank * 2048 + c0 * 2, bank * 2048 + c1 * 2)

    ring_state = {"i": 0}
    sc_cnt = {"p": 0, "v": 0, "y": 0, "acc": 0, "at": 0}

    def next_bank(nb=4):
        b = ring_state["i"] % nb
        ring_state["i"] += 1
        return b

    def DR(ap, name, lo=0, hi=1):
        return Reg(ap, "dram:" + name, lo, hi)

    def mm(out, lhsT, rhs, start=True, stop=True, extra_reads=()):
        add("pe", lambda e: e.matmul(out.ap, lhsT=lhsT.ap, rhs=rhs.ap, start=start, stop=stop,
                                     skip_group_check=True),
            reads=[lhsT, rhs] + list(extra_reads) + ([] if start else [out]), writes=[out])

    def tr(out, in_, ident):
        add("pe", lambda e: e.transpose(out.ap, in_.ap, ident.ap), reads=[in_, ident], writes=[out])

    def act(out, in_, func, bias=0.0, scale=1.0, eng="act"):
        rd = [in_]
        b = bias
        s = scale
        if isinstance(bias, Reg):
            rd.append(bias)
            b = bias.ap
        if isinstance(scale, Reg):
            rd.append(scale)
            s = scale.ap
        add("act", lambda e: e.activation(out.ap, in_.ap, func, bias=b, scale=s), reads=rd, writes=[out])

    def tt(eng, out, a, b, op):
        add(eng, lambda e: e.tensor_tensor(out.ap, a.ap, b.ap, op), reads=[a, b], writes=[out])

    def ts(eng, out, a, s1, op0, s2=None, op1=None):
        rd = [a]
        v1 = s1
        v2 = s2
        if isinstance(s1, Reg):
            rd.append(s1)
            v1 = s1.ap
        if isinstance(s2, Reg):
            rd.append(s2)
            v2 = s2.ap
        if op1 is None:
            add(eng, lambda e: e.tensor_scalar(out.ap, a.ap, v1, None, op0), reads=rd, writes=[out])
        else:
            add(eng, lambda e: e.tensor_scalar(out.ap, a.ap, v1, v2, op0, op1), reads=rd, writes=[out])

    def stt(out, a, s, b, op0, op1, accum=None):
        rd = [a, b]
        v = s
        if isinstance(s, Reg):
            rd.append(s)
            v = s.ap
        wr = [out]
        if accum is not None:
            wr.append(accum)
            add("dve", lambda e: e.scalar_tensor_tensor(out.ap, a.ap, v, b.ap, op0, op1, accum_out=accum.ap),
                reads=rd, writes=wr)
        else:
            add("dve", lambda e: e.scalar_tensor_tensor(out.ap, a.ap, v, b.ap, op0, op1), reads=rd, writes=wr)

    def cp(eng, out, in_):
        if eng == "act":
            add("act", lambda e: e.copy(out.ap, in_.ap), reads=[in_], writes=[out])
        else:
            add(eng, lambda e: e.tensor_copy(out.ap, in_.ap), reads=[in_], writes=[out])

    def dma(eng, out, in_):
        add(eng, lambda e: e.dma_start(out=out.ap, in_=in_.ap), reads=[in_], writes=[out], dma=True)

    def memset(eng, out, v):
        add(eng, lambda e: e.memset(out.ap, v), writes=[out])

    def bc(reg, shape):
        return Reg(reg.ap.to_broadcast(list(shape)), reg.arena, reg.lo, reg.hi)

    memset("pool", onesF.all(), 1.0)
    add("pool", lambda e: e.affine_select(identF.all().ap, onesF.all().ap, [[-1, 128]], ALU.is_equal, 0.0,
                                          base=0, channel_multiplier=1),
        reads=[onesF.all()], writes=[identF.all()])
    add("pool", lambda e: e.affine_select(triF.all().ap, onesF.all().ap, [[1, 128]], ALU.is_ge, 0.0,
                                          base=0, channel_multiplier=-1),
        reads=[onesF.all()], writes=[triF.all()])
    add("pool", lambda e: e.affine_select(selF.all().ap, onesF.all().ap, [[0, 128]], ALU.is_equal, 0.0,
                                          base=-127, channel_multiplier=1),
        reads=[onesF.all()], writes=[selF.all()])
    cp("pool", identB.all(), identF.all())
    cp("pool", triB.all(), triF.all())
    cp("pool", onesB.all(), onesF.all())
    memset("pool", Eend[:, 0, :], 0.0)
    memset("pool", qpad.all(), 0.0)

    def cast_rows(dst, src, name, rows, step):
        for r0 in range(0, rows, step):
            dma("pool", DR(dst[r0:r0 + step, :], name, r0 // 256, (r0 + step) // 256), DR(src[r0:r0 + step, :], name + "_src"))

    import os
    SKIP = set(os.environ.get("KSKIP", "").split(","))
    def colload(dst, src_ap, name):
        add("sp", lambda e: e.dma_start(out=dst.ap, in_=src_ap.rearrange("(c p) -> p c", p=128),
                                        allow_slow_non_contiguous=True),
            reads=[DR(src_ap, name)], writes=[dst], dma=True)

    colload(gcol[:, 0, :], g1, "g1")
    colload(gcol[:, 1, :], g2, "g2")
    colload(gcol[:, 2, :], g_f, "g_f")
    add("sp", lambda e: e.dma_start(out=gcol[:, 3, 0:4].ap, in_=g_att.rearrange("(c p) -> p c", p=128),
                                    allow_slow_non_contiguous=True),
        reads=[DR(g_att, "g_att")], writes=[gcol[:, 3, 0:4]], dma=True)
    add("sp", lambda e: e.dma_start(out=gcol[:, 3, 4:8].ap, in_=g_conv.rearrange("(c p) -> p c", p=128),
                                    allow_slow_non_contiguous=True),
        reads=[DR(g_conv, "g_conv")], writes=[gcol[:, 3, 4:8]], dma=True)
    for j_ in range(3):
        add("sp", lambda e, j_=j_: e.dma_start(out=wcv[:, :, j_].ap, in_=w_conv[j_].rearrange("(c p) -> p c", p=128),
                                               allow_slow_non_contiguous=True),
            reads=[DR(w_conv, "w_conv")], writes=[wcv.all()], dma=True)
    add("sp", lambda e: e.dma_start(out=bfb.all().ap, in_=b_f.partition_broadcast(128)),
        reads=[DR(b_f, "b_f")], writes=[bfb.all()], dma=True)
    add("pool", lambda e: e.dma_start(out=wflb.all().ap,
                                      in_=w_in[:, 3 * ATT:3 * ATT + NH].rearrange("(c p) n -> p c n", p=128)),
        reads=[DR(w_in, "w_in")], writes=[wflb.all()], dma=True)

    cT = sbuf("cT", [128, 8, 4], F32)
    sT = sbuf("sT", [128, 8, 4], BF16)
    for s_ in range(4):
        add("sp", lambda e, s_=s_: e.dma_start(out=cT[:, :, s_].ap, in_=cvec[s_].rearrange("(c p) -> p c", p=128),
                                               allow_slow_non_contiguous=True),
            reads=[DR(cvec, "cvec")], writes=[cT.all()], dma=True)
    ctmp = sbuf("ctmp", [128, 8, 4], F32)
    act(ctmp.all(), cT.all(), AF.Exp, scale=-1.0)
    ts("dve", ctmp.all(), ctmp.all(), 1.0, ALU.add)
    add("dve", lambda e: e.reciprocal(ctmp.all().ap, ctmp.all().ap), reads=[ctmp.all()], writes=[ctmp.all()])
    tt("dve", sT.all(), cT.all(), ctmp.all(), ALU.mult)
    brow = Buf(u2[0:4, 0:1024].rearrange("p (a b) -> p a b", b=512), "u2", 0, [4, 2, 512], 4)
    wslot = {"i": 0}

    def wload(src_ap, name, eng="sp", key=0):
        s = wslot["i"] % 3
        wslot["i"] += 1
        dst = wring[:, s, :, :]
        add(eng, lambda e: e.dma_start(out=dst.ap, in_=src_ap), reads=[DR(src_ap, name, key, key + 1)],
            writes=[dst], dma=True)
        return s

    for blk in range(0 if "ada" in SKIP else 16):
        if blk < 12:
            src = w_ada[:, blk * 512:(blk + 1) * 512]
        else:
            src = w_adaf[:, (blk - 12) * 512:(blk - 11) * 512]
        s = wload(src.rearrange("(c p) n -> p c n", p=128), "w_ada_src", eng="pool")
        b = next_bank()
        for c in range(8):
            mm(PS(b, 0, 512, 0, 4), sT[:, c, :], wring[:, s, c, :], start=(c == 0), stop=(c == 7))
        bsrc = b_ada[blk * 512:(blk + 1) * 512] if blk < 12 else b_adaf[(blk - 12) * 512:(blk - 11) * 512]
        add("sp", lambda e, bsrc=bsrc, blk=blk: e.dma_start(out=brow[0:4, blk % 2, :].ap, in_=bsrc.partition_broadcast(4)),
            reads=[DR(b_ada, "b_ada")], writes=[brow[0:4, blk % 2, :]], dma=True)
        tt("dve", modr[0:4, blk * 512:(blk + 1) * 512], PS(b, 0, 512, 0, 4), brow[0:4, blk % 2, :], ALU.add)
    modcol = view(u2, "u2", 4096, [128, 8, 8, 4], F32)
    for v in range(8):
        b = next_bank()
        for c in range(8):
            tr(PS(b, c * 4, c * 4 + 4), modr[0:4, v * D + c * 128: v * D + (c + 1) * 128], identF[0:4, 0:4])
        add("dve", lambda e, b=b, v=v: e.tensor_copy(modcol[:, v, :, :].ap,
                                                     PS(b, 0, 32).ap.rearrange("p (c s) -> p c s", s=4)),
            reads=[PS(b, 0, 32)], writes=[modcol[:, v, :, :]])
    for s_ in range(3):
        for (dst, scv, gi) in ((0, 1, 0), (3, 4, 1), (6, 7, 2)):
            stt(modc[:, s_, dst, :], modcol[:, scv, :, s_], 1.0, gcol[:, gi, :], ALU.add, ALU.mult)
        for (dst, srcv) in ((1, 0), (2, 2), (4, 3), (5, 5), (7, 6)):
            cp("dve", modc[:, s_, dst, :], modcol[:, srcv, :, s_])

    if "cast" not in SKIP:
        cast_rows(win_b, w_in, "win_b", D, 256)
        cast_rows(wout_b, w_out, "wout_b", D, 512)
        cast_rows(wup_b, w_up, "wup_b", D, 256)
        cast_rows(wdn_b, w_down, "wdn_b", DFF, 1024)

    def wblock(w_ap, name, r0, c0, key):
        src = w_ap[r0:r0 + 1024, c0:c0 + 512].rearrange("(c p) n -> p c n", p=128)
        s = wslot["i"] % 3
        wslot["i"] += 1
        dst = wring[:, s, :, :]
        add("sp", lambda e: e.dma_start(out=dst.ap, in_=src), reads=[DR(w_ap, name, 0, 16)],
            writes=[dst], dma=True)
        return s

    def rms_bc(srcs, nfeat, W, out_rstd):
        b = next_bank()
        n = len(srcs)
        for i, s_ in enumerate(srcs):
            sl = sq[:, i % 2, 0:W]
            tt("pool" if i % 2 == 0 else "dve", sl, s_, s_, ALU.mult)
            mm(PS(b, 0, W), onesB.all(), sl, start=(i == 0), stop=(i == n - 1))
        act(out_rstd, PS(b, 0, W), AF.Ln, bias=small_eps, scale=1.0 / nfeat)
        act(out_rstd, out_rstd, AF.Exp, scale=-0.5)

    def rms_feed(b, i, n, src, W):
        sl = sq[:, i % 2, 0:W]
        tt("pool" if i % 2 == 0 else "dve", sl, src, src, ALU.mult)
        mm(PS(b, 0, W), onesB.all(), sl, start=(i == 0), stop=(i == n - 1))

    def rms_finish(b, nfeat, W, out_rstd):
        act(out_rstd, PS(b, 0, W), AF.Ln, bias=small_eps, scale=1.0 / nfeat)
        act(out_rstd, out_rstd, AF.Exp, scale=-0.5)

    memset("pool", small[:, 0:1], EPS)
    small_eps = small[:, 0:1]

    xstate = {}

    def load_x(x_src, t0, ntl, TW):
        for tl in range(ntl):
            dma("act", xin[0:TW, tl, :], DR(x_src[t0 + tl * 128: t0 + tl * 128 + TW, :], "x_src"))

    def run_sequence(si, x_src, ntok, hist_tiles, outs, is_sample, next_x=None):
        y_dst, k_dst, v_dst, lf_dst, conv_dst = outs
        ngroups = (ntok + GW - 1) // GW
        for g in range(ngroups):
            W = min(GW, ntok - g * GW)
            ntl = (W + 127) // 128
            TW = min(128, W)
            t0 = g * GW
            if xstate.get("loaded") != (si, g):
                load_x(x_src, t0, ntl, TW)
            xstate["loaded"] = None
            for c in range(8):
                b = next_bank()
                for tl in range(ntl):
                    tr(PS(b, tl * 128, tl * 128 + TW), xin[0:TW, tl, c * 128:(c + 1) * 128], identF[0:TW, 0:TW])
                cp("act" if c % 2 == 0 else "dve", xT[:, c, 0:W], PS(b, 0, W))
                if c >= 2:
                    rms_feed(7, c - 2, 8, xT[:, c - 2, 0:W], W)
            rms_feed(7, 6, 8, xT[:, 6, 0:W], W)
            rms_feed(7, 7, 8, xT[:, 7, 0:W], W)
            rms_finish(7, D, W, rstd[:, 0:W])
            for c in range(8):
                tt("dve", tmpA[:, c % 2, 0:W], xT[:, c, 0:W], rstd[:, 0:W], ALU.mult)
                act(hT[:, c, 0:W], tmpA[:, c % 2, 0:W], AF.Identity, bias=modc[:, si, 1, c:c + 1], scale=modc[:, si, 0, c:c + 1])
            if cfg.get("STAGE", 9) <= 0.2:
                continue
            bfl = next_bank()
            for tl in range(ntl):
                for c in range(8):
                    mm(PS(bfl, tl * NH, tl * NH + NH, 0, TW), hT[:, c, tl * 128: tl * 128 + TW], wflb[:, c, :],
                       start=(c == 0), stop=(c == 7))
            for tl in range(ntl):
                tt("dve", flx[0:TW, tl, :], PS(bfl, tl * NH, tl * NH + NH, 0, TW), bfb[0:TW, :], ALU.add)
            act(flx[0:TW, 0:ntl, :], flx[0:TW, 0:ntl, :], AF.Exp, scale=-1.0)
            act(flx[0:TW, 0:ntl, :], flx[0:TW, 0:ntl, :], AF.Ln, bias=1.0)
            if TW < 128:
                memset("dve", lfst.all(), 0.0)
            ts("dve", lfst[0:TW, 0:ntl, :], flx[0:TW, 0:ntl, :], -1.0, ALU.mult)
            for tl in range(ntl):
                dma("act", DR(lf_dst[t0 + tl * 128: t0 + tl * 128 + TW, :], "lf_dst", t0 + tl * 128, t0 + tl * 128 + TW),
                    lfst[0:TW, tl, :])
            bF = next_bank()
            mm(PS(bF, 0, ntl * NH), triF.all(), lfst[:, 0:ntl, :])
            mm(PS(bF, 64, 64 + ntl * NH), onesF.all(), lfst[:, 0:ntl, :])
            for tl in range(ntl):
                j = hist_tiles + (t0 // 128) + tl
                tt("dve", Ftok[:, j, :], PS(bF, tl * NH, tl * NH + NH), Eend[:, j, :], ALU.add)
                tt("dve", Eend[:, j + 1, :], PS(bF, 64 + tl * NH, 64 + tl * NH + NH), Eend[:, j, :], ALU.add)
                tt("dve", kfac[:, j, :], Eend[:, j + 1, :], Ftok[:, j, :], ALU.subtract)
            j0 = hist_tiles + t0 // 128
            act(kfac[:, j0:j0 + ntl, :], kfac[:, j0:j0 + ntl, :], AF.Exp)
            if cfg.get("STAGE", 9) <= 0.3:
                continue
            s = wblock(win_b, "win_b", 0, 0, 0)
            for p in range(4):
                b = next_bank()
                for c in range(8):
                    mm(PS(b, 0, W), wring[:, s, c, p * 128:(p + 1) * 128], hT[:, c, 0:W], start=(c == 0), stop=(c == 7))
                cp("act", qpad[0:64, 2 * p, 0:W], PS(b, 0, W, 0, 64))
                cp("act", qpad[64:128, 2 * p + 1, 0:W], PS(b, 0, W, 64, 128))
            if cfg.get("STAGE", 9) <= 0.4:
                continue
            s = wblock(win_b, "win_b", 0, ATT, 1)
            for tl in range(0 if "kpart" in SKIP else ntl):
                j = j0 + tl
                b = next_bank()
                for c in range(8):
                    mm(PS(b, 0, 512, 0, TW), hT[:, c, tl * 128: tl * 128 + TW], wring[:, s, c, :], start=(c == 0), stop=(c == 7))
                cp("act", kst[0:TW, tl, :], PS(b, 0, 512, 0, TW))
                dma("act", DR(k_dst[t0 + tl * 128: t0 + tl * 128 + TW, :], "k_dst", t0 + tl * 128, t0 + tl * 128 + TW),
                    kst[0:TW, tl, :])
                b2 = next_bank()
                for p in range(4):
                    tr(PS(b2, p * 128, p * 128 + TW), kst[0:TW, tl, p * 128:(p + 1) * 128], identF[0:TW, 0:TW])
                add("dve", lambda e, b2=b2, j=j, TW=TW: e.tensor_copy(
                    kT[:, :, j * 128: j * 128 + TW].ap,
                    psum_t[b2][:, :].rearrange("p (a t) -> p a t", t=128)[:, :, 0:TW]),
                    reads=[PS(b2, 0, 512)], writes=[kT[:, p_, j * 128: j * 128 + TW] for p_ in range(4)])
            if cfg.get("STAGE", 9) <= 0.5:
                continue
            s = wblock(win_b, "win_b", 0, 2 * ATT, 2)
            for tl in range(ntl):
                j = j0 + tl
                b = next_bank()
                for c in range(8):
                    mm(PS(b, 0, 512, 0, TW), hT[:, c, tl * 128: tl * 128 + TW], wring[:, s, c, :], start=(c == 0), stop=(c == 7))
                cp("act", vst[0:TW, tl, :], PS(b, 0, 512, 0, TW))
                if "vdma" not in SKIP:
                    dma("act", DR(v_dst[t0 + tl * 128: t0 + tl * 128 + TW, :], "v_dst", t0 + tl * 128, t0 + tl * 128 + TW),
                        vst[0:TW, tl, :])
                if "vops" in SKIP:
                    continue
                if TW < 128:
                    memset("pool", Vp[:, j, :, :], 0.0)
                if "vtt" in SKIP:
                    for h_ in range(NH):
                        ts("dve", Vp[0:TW, j, h_, 0:64], vst[0:TW, tl, h_ * 64:(h_ + 1) * 64], kfac[0:TW, j, h_:h_ + 1], ALU.mult)
                else:
                    add("dve", lambda e, tl=tl, j=j, TW=TW: e.tensor_tensor(
                        Vp[0:TW, j, :, 0:64].ap,
                        vst[0:TW, tl, :].ap.rearrange("p (h d) -> p h d", d=64),
                        kfac[0:TW, j, :].ap.unsqueeze(2).to_broadcast([TW, NH, 64]), ALU.mult),
                        reads=[vst[0:TW, tl, :], kfac[0:TW, j, :]], writes=[Vp[0:TW, j, :, :]])
                if "vp1" not in SKIP:
                    add("pool", lambda e, j=j, TW=TW: e.tensor_copy(Vp[0:TW, j, :, 64:65].ap, kfac[0:TW, j, :].ap.unsqueeze(2)),
                        reads=[kfac[0:TW, j, :]], writes=[Vp[0:TW, j, :, :]])
            if cfg.get("STAGE", 9) <= 1:
                continue
            LOOK = 2
            NQ = ntl
            TWq = TW
            for ii in range(NQ):
                i = j0 + ii
                add("dve", lambda e, i=i, ii=ii: e.tensor_tensor(
                    expD[:, ii, 0:i + 1, :].ap, Eend[:, i + 1:i + 2, :].ap.to_broadcast([128, i + 1, NH]),
                    Eend[:, 1:i + 2, :].ap, ALU.subtract),
                    reads=[Eend[:, i + 1, :], Eend[:, 1:i + 2, :]], writes=[expD[:, ii, 0:i + 1, :]])
                act(expD[:, ii, 0:i + 1, :], expD[:, ii, 0:i + 1, :], AF.Exp)
            accb = [4, 5, 6, 7]
            nk = j0 + NQ
            for hh in range(2):
                steps = [(hl, j) for hl in range(4) for j in range(nk)]
                info = {}

                def front(sidx, hh=hh, steps=steps, info=info):
                    hl, j = steps[sidx]
                    h = 2 * hl + hh
                    p = h // 2
                    r0 = (h % 2) * 64
                    iim = max(0, j - j0)
                    TWk = TWq if j == nk - 1 else 128
                    c0_ = iim * 128
                    bS = next_bank()
                    mm(PS(bS, c0_, W, 0, TWk), kT[:, p, j * 128: j * 128 + TWk], qpad[:, h, c0_:W])
                    slot = sc_cnt["p"] % 4
                    sc_cnt["p"] += 1
                    act(Pr[0:TWk, slot, c0_:W], PS(bS, c0_, W, 0, TWk), AF.Exp, scale=ATT_SCALE)
                    if j >= j0:
                        tt("pool", Pr[0:TWk, slot, c0_:c0_ + TWq], Pr[0:TWk, slot, c0_:c0_ + TWq], triB[0:TWk, 0:TWq], ALU.mult)
                    vslot = sc_cnt["v"] % 4
                    sc_cnt["v"] += 1
                    n_ = NQ - iim
                    veng = "dve"
                    add(veng, lambda e, vslot=vslot, j=j, h=h, iim=iim, n_=n_: e.tensor_tensor(
                        Vs[:, vslot, iim:NQ, :].ap,
                        Vp[:, j, h, 0:65].ap.unsqueeze(1).to_broadcast([128, n_, 65]),
                        expD[:, iim:NQ, j, h].ap.unsqueeze(2).to_broadcast([128, n_, 65]), ALU.mult),
                        reads=[Vp[:, j, h, :], expD[:, iim:NQ, j, h]], writes=[Vs[:, vslot, iim:NQ, :]])
                    info[sidx] = (slot, vslot, TWk, iim)

                def back(sidx, hh=hh, steps=steps, info=info):
                    hl, j = steps[sidx]
                    slot, vslot, TWk, iim = info[sidx]
                    for ii in range(iim, NQ):
                        mm(PS(accb[ii], hl * 65, hl * 65 + 65, 0, TWq), Pr[0:TWk, slot, ii * 128: ii * 128 + TWq],
                           Vs[0:TWk, vslot, ii, 0:65], start=(j == 0 and hl == 0), stop=(j == j0 + ii and hl == 3))

                if is_sample and NQ == 1 and TWq <= 16:
                    Vsb = Buf(Vs.ap.rearrange("p a b d -> p (a b) d"), "vsraw", 0, [128, 16, 65], 2)
                    for hl in range(4):
                        h = 2 * hl + hh
                        p = h // 2
                        chunks = [list(range(a, min(a + 16, j0))) for a in range(0, j0, 16)] + [[j0]]
                        for ch in chunks:
                            own = (ch[0] == j0)
                            TWk = TWq if own else 128
                            n_ = len(ch)
                            bS = next_bank()
                            for jj, j in enumerate(ch):
                                mm(PS(bS, jj * 16, jj * 16 + TWq, 0, TWk), kT[:, p, j * 128: j * 128 + TWk], qpad[:, h, 0:TWq])
                            slot = sc_cnt["p"] % 4
                            sc_cnt["p"] += 1
                            if TWq == 16:
                                act(Pr[0:TWk, slot, 0:n_ * 16], PS(bS, 0, n_ * 16, 0, TWk), AF.Exp, scale=ATT_SCALE)
                            else:
                                add("act", lambda e, bS=bS, slot=slot, TWk=TWk, n_=n_: e.activation(
                                    Pr[0:TWk, slot, 0:n_ * 16].ap.rearrange("p (a t) -> p a t", t=16)[:, :, 0:TWq],
                                    psum_t[bS][0:TWk, 0:n_ * 16].rearrange("p (a t) -> p a t", t=16)[:, :, 0:TWq],
                                    AF.Exp, scale=ATT_SCALE),
                                    reads=[PS(bS, 0, n_ * 16, 0, TWk)], writes=[Pr[0:TWk, slot, 0:n_ * 16]])
                            if own:
                                tt("pool", Pr[0:TWk, slot, 0:TWq], Pr[0:TWk, slot, 0:TWq], triB[0:TWk, 0:TWq], ALU.mult)
                            ja, jb = ch[0], ch[-1] + 1
                            add("dve", lambda e, ja=ja, jb=jb, h=h, n_=n_: e.tensor_tensor(
                                Vsb[:, 0:n_, :].ap, Vp[:, ja:jb, h, 0:65].ap,
                                expD[:, 0, ja:jb, h].ap.unsqueeze(2).to_broadcast([128, n_, 65]), ALU.mult),
                                reads=[Vp[:, ja:jb, h, :], expD[:, 0, ja:jb, h]], writes=[Vsb[:, 0:n_, :]])
                            for jj, j in enumerate(ch):
                                mm(PS(accb[0], hl * 65, hl * 65 + 65, 0, TWq), Pr[0:TWk, slot, jj * 16: jj * 16 + TWq],
                                   Vsb[0:TWk, jj, 0:65], start=(j == 0 and hl == 0), stop=(j == j0 and hl == 3))
                else:
                    n_st = len(steps)
                    for sidx in range(n_st + LOOK):
                        if sidx < n_st:
                            front(sidx)
                        if sidx - LOOK >= 0:
                            back(sidx - LOOK)
                for ii in range(NQ):
                    cb = 8 + (ii * 2 + hh) * 4
                    add("dve", lambda e, cb=cb, ab=accb[ii]: e.reciprocal(
                        small[0:TWq, cb:cb + 4].ap,
                        psum_t[ab][0:TWq, 0:260].rearrange("p (h d) -> p h d", d=65)[:, :, 64]),
                        reads=[PS(accb[ii], 0, 260, 0, TWq)], writes=[small[0:TWq, cb:cb + 4]])
                    add("dve", lambda e, cb=cb, ab=accb[ii], ii=ii, hh=hh: e.tensor_tensor(
                        att0[0:TWq, ii, :].ap.rearrange("p (h q d) -> p h q d", q=2, d=64)[:, :, hh, :],
                        psum_t[ab][0:TWq, 0:260].rearrange("p (h d) -> p h d", d=65)[:, :, 0:64],
                        small[0:TWq, cb:cb + 4].ap.unsqueeze(2).to_broadcast([TWq, 4, 64]), ALU.mult),
                        reads=[PS(accb[ii], 0, 260, 0, TWq), small[0:TWq, cb:cb + 4]],
                        writes=[att0[0:TWq, ii, :]])
            for ii in range(NQ):
                stt(tmpA[0:TWq, ii % 2, 0:ATT], att0[0:TWq, ii, :], 1.0, att0[0:TWq, ii, :], ALU.mult, ALU.mult,
                    accum=small[0:TWq, 40 + ii:41 + ii])
                act(small[0:TWq, 44 + ii:45 + ii], small[0:TWq, 40 + ii:41 + ii], AF.Ln, bias=small[0:TWq, 0:1], scale=1.0 / ATT)
                act(small[0:TWq, 48 + ii:49 + ii], small[0:TWq, 44 + ii:45 + ii], AF.Exp, scale=-0.5)
                ts("dve", attn[0:TWq, ii, :], att0[0:TWq, ii, :], small[0:TWq, 48 + ii:49 + ii], ALU.mult)
            sA = wblock(win_b, "win_b", 0, 3 * ATT + NH, 3)
            sB = wblock(win_b, "win_b", 0, 3 * ATT + NH + CONV, 4)
            sC = wblock(win_b, "win_b", 0, 3 * ATT + NH + 2 * CONV, 5)
            for cc in range(4):
                b = next_bank()
                for c in range(8):
                    mm(PS(b, 0, W), wring[:, sC, c, cc * 128:(cc + 1) * 128], hT[:, c, 0:W], start=(c == 0), stop=(c == 7))
                cp("act", usb[:, 0:W], PS(b, 0, W))
                b = next_bank()
                for c in range(8):
                    mm(PS(b, 0, W), wring[:, sB, c, cc * 128:(cc + 1) * 128], hT[:, c, 0:W], start=(c == 0), stop=(c == 7))
                if g == 0:
                    if is_sample:
                        for j_ in range(2):
                            add("sp", lambda e, j_=j_, cc=cc: e.dma_start(
                                out=cu[:, cc, j_:j_ + 1].ap,
                                in_=cconv[j_, cc * 128:(cc + 1) * 128].rearrange("(p o) -> p o", o=1),
                                allow_slow_non_contiguous=True),
                                reads=[DR(cconv, "cconv")], writes=[cu[:, cc, j_:j_ + 1]], dma=True)
                    else:
                        memset("pool", cu[:, cc, 0:2], 0.0)
                else:
                    cp("pool", cu[:, cc, 0:2], chist[:, cc, :])
                tt("dve", cu[:, cc, 2:2 + W], PS(b, 0, W), usb[:, 0:W], ALU.mult)
                cp("pool", chist[:, cc, :], cu[:, cc, W:W + 2])
                ts("dve", ycv[:, 0:W], cu[:, cc, 0:W], wcv[:, cc, 0:1], ALU.mult)
                stt(ycv[:, 0:W], cu[:, cc, 1:1 + W], wcv[:, cc, 1:2], ycv[:, 0:W], ALU.mult, ALU.add)
                stt(ycv[:, 0:W], cu[:, cc, 2:2 + W], wcv[:, cc, 2:3], ycv[:, 0:W], ALU.mult, ALU.add)
                b = next_bank()
                for c in range(8):
                    mm(PS(b, 0, W), wring[:, sA, c, cc * 128:(cc + 1) * 128], hT[:, c, 0:W], start=(c == 0), stop=(c == 7))
                tt("dve", zT[:, cc, 0:W], PS(b, 0, W), ycv[:, 0:W], ALU.mult)
            rms_bc([zT[:, cc, 0:W] for cc in range(4)], CONV, W, rstd[:, 0:W])
            for cc in range(4):
                tt("dve", tmpA[:, cc % 2, 0:W], zT[:, cc, 0:W], rstd[:, 0:W], ALU.mult)
                act(mT[:, 4 + cc, 0:W], tmpA[:, cc % 2, 0:W], AF.Identity, scale=gcol[:, 3, 4 + cc:5 + cc])
            if g == ngroups - 1:
                for j_ in range(2):
                    for cc in range(4):
                        add("sp", lambda e, j_=j_, cc=cc: e.dma_start(
                            out=conv_dst[j_, cc * 128:(cc + 1) * 128].rearrange("(p o) -> p o", o=1),
                            in_=chist[:, cc, j_:j_ + 1].ap, allow_slow_non_contiguous=True),
                            reads=[chist[:, cc, j_:j_ + 1]], writes=[DR(conv_dst, "conv_dst", j_ * 4 + cc, j_ * 4 + cc + 1)], dma=True)
            for ii in range(NQ):
                bT = next_bank()
                for c in range(4):
                    tr(PSB(bT, c * 128, c * 128 + TWq), attn[0:TWq, ii, c * 128:(c + 1) * 128], identB[0:TWq, 0:TWq])
                for c in range(4):
                    ts("dve", mT[:, c, ii * 128: ii * 128 + TWq], PSB(bT, c * 128, c * 128 + TWq),
                       gcol[:, 3, c:c + 1], ALU.mult)
            if cfg.get("STAGE", 9) <= 2:
                continue
            so = [wblock(wout_b, "wout_b", 0, 0, 0), wblock(wout_b, "wout_b", 0, 512, 1)]
            for cc in range(8):
                b = next_bank()
                for c in range(8):
                    mm(PS(b, 0, W), wring[:, so[cc // 4], c, (cc % 4) * 128:(cc % 4 + 1) * 128], mT[:, c, 0:W],
                       start=(c == 0), stop=(c == 7))
                stt(xT[:, cc, 0:W], PS(b, 0, W), modc[:, si, 2, cc:cc + 1], xT[:, cc, 0:W], ALU.mult, ALU.add)
                if cc >= 2:
                    rms_feed(7, cc - 2, 8, xT[:, cc - 2, 0:W], W)
            if g + 1 < ngroups:
                W2 = min(GW, ntok - (g + 1) * GW)
                load_x(x_src, (g + 1) * GW, (W2 + 127) // 128, min(128, W2))
                xstate["loaded"] = (si, g + 1)
            elif next_x is not None:
                nsi, nsrc, nntok = next_x
                W2 = min(GW, nntok)
                load_x(nsrc, 0, (W2 + 127) // 128, min(128, W2))
                xstate["loaded"] = (nsi, 0)
            rms_feed(7, 6, 8, xT[:, 6, 0:W], W)
            rms_feed(7, 7, 8, xT[:, 7, 0:W], W)
            rms_finish(7, D, W, rstd[:, 0:W])
            for c in range(8):
                tt("dve", tmpA[:, c % 2, 0:W], xT[:, c, 0:W], rstd[:, 0:W], ALU.mult)
                act(hT[:, c, 0:W], tmpA[:, c % 2, 0:W], AF.Identity, bias=modc[:, si, 4, c:c + 1], scale=modc[:, si, 3, c:c + 1])
            for blk in range(8):
                s = wblock(wup_b, "wup_b", 0, blk * 512, blk)
                for q in range(4):
                    ffc = blk * 4 + q
                    b = next_bank()
                    for c in range(8):
                        mm(PS(b, 0, W), wring[:, s, c, q * 128:(q + 1) * 128], hT[:, c, 0:W], start=(c == 0), stop=(c == 7))
                    ts("dve", rbuf[:, ffc % 2, 0:W], PS(b, 0, W), 0.0, ALU.max)
                    tt("pool", aT[:, ffc, 0:W], rbuf[:, ffc % 2, 0:W], rbuf[:, ffc % 2, 0:W], ALU.mult)
            for half in range(2):
                banks = [4, 5, 6, 7] if half == 0 else [0, 1, 2, 3]
                for oc in range(4):
                    s = wblock(wdn_b, "wdn_b", oc * 1024, half * 512, half * 4 + oc)
                    if half == 1 and oc >= 1:
                        for c_ in ((0, 1), (2,), (3,))[oc - 1]:
                            rms_feed(4, c_, 8, xT[:, c_, 0:W], W)
                    for cc in range(4):
                        for fl_ in range(8):
                            mm(PS(banks[cc], 0, W), wring[:, s, fl_, cc * 128:(cc + 1) * 128], aT[:, oc * 8 + fl_, 0:W],
                               start=(oc == 0 and fl_ == 0), stop=(oc == 3 and fl_ == 7))
                for cc in range(4):
                    c = half * 4 + cc
                    stt(xT[:, c, 0:W], PS(banks[cc], 0, W), modc[:, si, 5, c:c + 1], xT[:, c, 0:W], ALU.mult, ALU.add)
            for c_ in range(4, 8):
                rms_feed(4, c_, 8, xT[:, c_, 0:W], W)
            rms_finish(4, D, W, rstd[:, 0:W])
            for c in range(8):
                tt("dve", tmpA[:, c % 2, 0:W], xT[:, c, 0:W], rstd[:, 0:W], ALU.mult)
                act(xT[:, c, 0:W], tmpA[:, c % 2, 0:W], AF.Identity, bias=modc[:, si, 7, c:c + 1], scale=modc[:, si, 6, c:c + 1])
            for tl in range(ntl):
                ysl = sc_cnt["y"] % 2
                sc_cnt["y"] += 1
                for half in range(2):
                    b = next_bank()
                    for q in range(4):
                        c = half * 4 + q
                        tr(PS(b, q * 128, (q + 1) * 128, 0, TW), xT[:, c, tl * 128: tl * 128 + TW], identF.all())
                    cp("act" if half == 0 else "dve", yst[0:TW, ysl, half * 512:(half + 1) * 512], PS(b, 0, 512, 0, TW))
                dma("act", DR(y_dst[t0 + tl * 128: t0 + tl * 128 + TW, :], "y_dst", t0 + tl * 128, t0 + tl * 128 + TW),
                    yst[0:TW, ysl, :])
    for s_ in range(NSEQ if cfg.get("STAGE", 9) >= 1 else 0):
        nx = (s_ + 1, xp[s_ + 1], S) if s_ + 1 < NSEQ else None
        run_sequence(s_, xp[s_], S, 0, (yp[s_], kp[s_], vp[s_], lfp[s_], convp[s_]), False, next_x=nx)

    if SAMPLE:
        PT = PAST // 128
        lfc = view(u1, "u1", 30720, [128, PT, NH], F32)
        add("sp", lambda e: e.dma_start(out=lfc.all().ap, in_=clf.rearrange("(t p) h -> p t h", p=128)),
            reads=[DR(clf, "clf")], writes=[lfc.all()], dma=True)
        bF = next_bank()
        mm(PS(bF, 0, PT * NH), triF.all(), lfc.all())
        bG = next_bank()
        mm(PS(bG, 0, PT * NH), onesF.all(), lfc.all())
        for j in range(PT):
            tt("dve", Eend[:, j + 1, :], PS(bG, j * NH, (j + 1) * NH), Eend[:, j, :], ALU.add)
        add("dve", lambda e: e.tensor_tensor(Ftok[:, 0:PT, :].ap,
                                             psum_t[bF][:, 0:PT * NH].rearrange("p (t h) -> p t h", h=NH),
                                             Eend[:, 0:PT, :].ap, ALU.add),
            reads=[PS(bF, 0, PT * NH), Eend[:, 0:PT, :]], writes=[Ftok[:, 0:PT, :]])
        tt("dve", kfac[:, 0:PT, :], Eend[:, 1:PT + 1, :], Ftok[:, 0:PT, :], ALU.subtract)
        act(kfac[:, 0:PT, :], kfac[:, 0:PT, :], AF.Exp)
        for j in range(PT):
            sl = j % 4
            dma("sp", kst[:, sl, :], DR(ck[j * 128:(j + 1) * 128, :], "ck"))
            dma("sp", vst[:, sl, :], DR(cv[j * 128:(j + 1) * 128, :], "cv"))
            b2 = next_bank()
            for p in range(4):
                tr(PS(b2, p * 128, (p + 1) * 128), kst[:, sl, p * 128:(p + 1) * 128], identF.all())
            add("dve", lambda e, b2=b2, j=j: e.tensor_copy(
                kT[:, :, j * 128:(j + 1) * 128].ap, psum_t[b2][:, :].rearrange("p (a t) -> p a t", t=128)),
                reads=[PS(b2, 0, 512)], writes=[kT[:, p_, j * 128:(j + 1) * 128] for p_ in range(4)])
            add("pool", lambda e, sl=sl, j=j: e.tensor_tensor(
                Vp[:, j, :, 0:64].ap, vst[:, sl, :].ap.rearrange("p (h d) -> p h d", d=64),
                kfac[:, j, :].ap.unsqueeze(2).to_broadcast([128, NH, 64]), ALU.mult),
                reads=[vst[:, sl, :], kfac[:, j, :]], writes=[Vp[:, j, :, :]])
            add("pool", lambda e, j=j: e.tensor_copy(Vp[:, j, :, 64:65].ap, kfac[:, j, :].ap.unsqueeze(2)),
                reads=[kfac[:, j, :]], writes=[Vp[:, j, :, :]])
        run_sequence(2, xs, T, PT, (ys, ks, vs, lfs, convs), True)

    print("sbuf bytes remaining", nc.sbuf_bytes_remaining)
    sc.emit()
    stack.__exit__(None, None, None)
    return nc, sc


def make_in_map(inp, core, cfg):
    NSEQ = cfg["NSEQ"]
    f = lambda a: np.ascontiguousarray(np.asarray(a, dtype=np.float32))
    cvec = np.zeros((4, D), np.float32)
    cvec[0:NSEQ] = inp["c_prompt"][core * NSEQ:(core + 1) * NSEQ]
    cvec[2] = inp["c_sample"][core]
    m = {
        "xp": f(inp["x_prompt"][core * NSEQ:(core + 1) * NSEQ]),
        "cvec": cvec,
        "xs": f(inp["x_sample"][core]),
        "ck": f(inp["cache_k"][0, core]).reshape(-1, ATT),
        "cv": f(inp["cache_v"][0, core]).reshape(-1, ATT),
        "clf": f(inp["cache_logf"][0, core]),
        "cconv": f(inp["cache_conv"][0, core]),
        "w_ada": f(inp["w_ada"][0]), "b_ada": f(inp["b_ada"][0]),
        "g1": f(inp["g_norm1"][0]), "g2": f(inp["g_norm2"][0]),
        "w_in": f(inp["w_in"][0]), "b_f": f(inp["b_f"][0]), "w_conv": f(inp["w_conv"][0]),
        "g_att": f(inp["g_attn_out"][0]), "g_conv": f(inp["g_conv_out"][0]),
        "w_out": f(inp["w_out"][0]), "w_up": f(inp["w_up"][0]), "w_down": f(inp["w_down"][0]),
        "w_adaf": f(inp["w_ada_final"]), "b_adaf": f(inp["b_ada_final"]), "g_f": f(inp["g_final"]),
    }
    return m


_CFG = dict(S=4096, NSEQ=2, P=4096, T=16, SAMPLE=True)


def kernel(**inputs):
    cfg = dict(_CFG)
    inp = {k: np.asarray(v) for k, v in inputs.items()}
    nc, _ = build_program(cfg)
    in_maps = [make_in_map(inp, c, cfg) for c in range(NCORES)]
    res = run_bass_kernel_spmd(nc, in_maps, core_ids=list(range(NCORES)))
    r = res.results
    S, T = cfg["S"], cfg["T"]
    cat = lambda name: np.concatenate([np.asarray(r[c][name], dtype=np.float32) for c in range(NCORES)], axis=0)
    st = lambda name: np.stack([np.asarray(r[c][name], dtype=np.float32) for c in range(NCORES)], axis=0)
    y_prompt = cat("yp").reshape(16, S, D)
    y_sample = st("ys").reshape(8, T, D)
    kp = cat("kp").reshape(1, 16, S, NH, HD)
    vp = cat("vp").reshape(1, 16, S, NH, HD)
    lfp = cat("lfp").reshape(1, 16, S, NH)
    convp = cat("convp").reshape(1, 16, 2, CONV)
    ks = st("ks").reshape(1, 8, T, NH, HD)
    vs = st("vs").reshape(1, 8, T, NH, HD)
    lfs = st("lfs").reshape(1, 8, T, NH)
    convs = st("convs").reshape(1, 8, 2, CONV)
    return (y_prompt, y_sample, kp, vp, lfp, convp, ks, vs, lfs, convs)
```

```python
import numpy as np
import concourse.bass as bass
import concourse.mybir as mybir
from concourse.bass_utils import run_bass_kernel_spmd

F32 = mybir.dt.float32
BF16 = mybir.dt.bfloat16
AF = mybir.ActivationFunctionType
ALU = mybir.AluOpType

D = 1024
NH = 8
HD = 64
ATT = 512
CONV = 512
DFF = 4096
PROJ = 3080
EPS = 1e-6
ATT_SCALE = HD ** -0.5
NCORES = 8

ENGS = ("pe", "act", "dve", "pool", "sp")


class Reg:
    __slots__ = ("ap", "arena", "lo", "hi")

    def __init__(self, ap, arena, lo, hi):
        self.ap, self.arena, self.lo, self.hi = ap, arena, lo, hi


class Buf:
    def __init__(self, ap, arena, base, shape, esize):
        self.ap, self.arena, self.base, self.shape, self.esize = ap, arena, base, tuple(shape), esize
        st = []
        s = 1
        for d in reversed(self.shape[1:]):
            st.append(s)
            s *= d
        self.strides = tuple(reversed(st))
        self.row = s

    def __getitem__(self, key):
        if not isinstance(key, tuple):
            key = (key,)
        key = key + (slice(None),) * (len(self.shape) - len(key))
        lo = 0
        hi = 0
        for k, d, st in zip(key[1:], self.shape[1:], self.strides):
            if isinstance(k, slice):
                a = 0 if k.start is None else k.start
                b = d if k.stop is None else k.stop
                assert 0 <= a < b <= d, (key, self.shape)
            else:
                a, b = k, k + 1
                assert 0 <= a < d, (key, self.shape)
            lo += a * st
            hi += (b - 1) * st
        hi += 1
        return Reg(self.ap[key], self.arena, self.base + lo * self.esize, self.base + hi * self.esize)

    def all(self):
        return self[tuple(slice(None) for _ in self.shape)]


def esize_of(dt):
    return 2 if dt == BF16 else 4


class Sched:
    def __init__(self, nc):
        self.nc = nc
        self.ops = []
        self.cells = {}
        self.gran = {}
        self.stack = None

    def cellrange(self, r):
        g = self.gran.get(r.arena)
        if g is None:
            g = 2048 if r.arena == "psum" else (1 if r.arena.startswith("dram") else 128)
            self.gran[r.arena] = g
        return range(r.lo // g, (r.hi - 1) // g + 1)

    def add(self, eng, fn, reads=(), writes=(), dma=False):
        i = len(self.ops)
        deps = set()
        writes = list(writes) + [r for r in reads if r.arena == "psum"]
        for r in reads:
            for c in self.cellrange(r):
                st = self.cells.get((r.arena, c))
                if st is None:
                    st = [None, {}, []]
                    self.cells[(r.arena, c)] = st
                if st[0] is not None:
                    deps.add(st[0])
        for r in writes:
            for c in self.cellrange(r):
                st = self.cells.get((r.arena, c))
                if st is None:
                    st = [None, {}, []]
                    self.cells[(r.arena, c)] = st
                if st[0] is not None:
                    deps.add(st[0])
                deps.update(st[1].values())
                deps.update(st[2])
        for r in reads:
            for c in self.cellrange(r):
                st = self.cells[(r.arena, c)]
                if dma:
                    st[2].append(i)
                else:
                    st[1][eng] = i
        for r in writes:
            for c in self.cellrange(r):
                st = self.cells[(r.arena, c)]
                st[0] = i
                st[1] = {}
                st[2] = []
        deps.discard(i)
        self.ops.append(dict(eng=eng, fn=fn, dma=dma, deps=deps,
                             dbg=([(r.arena, r.lo, r.hi) for r in reads], [(r.arena, r.lo, r.hi) for r in writes])))
        return i

    def emit(self, n_dma_sems=24):
        nc = self.nc
        ops = self.ops
        for o in ops:
            o["sig"] = False
        for o in ops:
            need = []
            for d in o["deps"]:
                p = ops[d]
                if p["dma"] or p["eng"] != o["eng"] or o["eng"] != "pe":
                    need.append(d)
                    if not p["dma"]:
                        p["sig"] = True
            o["need"] = need
        tick = {e: 0 for e in ENGS}
        for o in ops:
            if o["dma"]:
                continue
            if o["sig"]:
                tick[o["eng"]] += 1
                o["tick"] = tick[o["eng"]]
        dma_cnt = [0] * n_dma_sems
        rng = {"pool": (0, 3), "sp": (8, 16), "act": (16, n_dma_sems)}
        rr = {"pool": 0, "sp": 0, "act": 0}
        for o in ops:
            if o["dma"]:
                a, b_ = rng[o["eng"]]
                s = a + rr[o["eng"]] % (b_ - a)
                rr[o["eng"]] += 1
                o["dsem"] = s
                o["dprev"] = dma_cnt[s]
                dma_cnt[s] += 16
                o["dval"] = dma_cnt[s]
        seen = {e: {x: 0 for x in ENGS} for e in ENGS}
        seen_d = {e: [0] * n_dma_sems for e in ENGS}
        per_eng = {e: [] for e in ENGS}
        for o in ops:
            e = o["eng"]
            waits = []
            best = {}
            for d in o["need"]:
                p = ops[d]
                if p["dma"]:
                    if seen_d[e][p["dsem"]] < p["dval"]:
                        seen_d[e][p["dsem"]] = p["dval"]
                        waits.append(("d", p["dsem"], p["dval"]))
                else:
                    if best.get(p["eng"], (0, None))[0] < p["tick"]:
                        best[p["eng"]] = (p["tick"], p)
            for pe_, (t, p) in best.items():
                if seen[e][pe_] < t:
                    waits.append(("c", pe_, t))
                    seen[e][pe_] = t
                    for x, v in p["snap"].items():
                        if seen[e][x] < v:
                            seen[e][x] = v
            if o["dma"]:
                if seen_d[e][o["dsem"]] < o["dprev"]:
                    seen_d[e][o["dsem"]] = o["dprev"]
                    waits.append(("d", o["dsem"], o["dprev"]))
            o["waits"] = waits
            if (not o["dma"]) and o["sig"]:
                o["snap"] = dict(seen[e])
                o["snap"][e] = o["tick"]
            per_eng[e].append(o)
        self.stats = {e: len(per_eng[e]) for e in ENGS}
        self.stats["ticks"] = dict(tick)
        final_dma = list(dma_cnt)

        import contextlib

        with contextlib.ExitStack() as st:
            csem = {e: st.enter_context(nc.semaphore("c_" + e)) for e in ENGS if e != "sp"}
            dsem = [st.enter_context(nc.semaphore("d%d" % i)) for i in range(n_dma_sems)]
            block = st.enter_context(nc.Block())

            def run(engname, e):
                for o in per_eng[engname]:
                    for w in o["waits"]:
                        if w[0] == "c":
                            e.wait_ge(csem[w[1]], w[2])
                        else:
                            e.wait_ge(dsem[w[1]], w[2])
                    ins = o["fn"](e)
                    if o["dma"]:
                        ins.then_inc(dsem[o["dsem"]], 16)
                    elif o["sig"]:
                        ins.then_inc(csem[engname], 1)
                if engname == "sp":
                    for i, v in enumerate(final_dma):
                        if v:
                            e.wait_ge(dsem[i], v)

            @block.tensor
            def _(e):
                run("pe", e)

            @block.scalar
            def _(e):
                run("act", e)

            @block.vector
            def _(e):
                run("dve", e)

            @block.gpsimd
            def _(e):
                run("pool", e)

            @block.sync
            def _(e):
                run("sp", e)


def build_program(cfg):
    S = cfg["S"]
    NSEQ = cfg["NSEQ"]
    PAST = cfg["P"]
    T = cfg["T"]
    SAMPLE = cfg.get("SAMPLE", True)
    GW = 512
    nc = bass.Bass("TRN2", target_bir_lowering=False)
    sc = Sched(nc)
    add = sc.add

    def din(name, shape, dt=F32):
        return nc.dram_tensor(name, list(shape), dt, kind="ExternalInput").ap()

    def dout(name, shape, dt=F32):
        return nc.dram_tensor(name, list(shape), dt, kind="ExternalOutput").ap()

    def dscr(name, shape, dt):
        return nc.dram_tensor(name, list(shape), dt, kind="Internal").ap()

    xp = din("xp", [NSEQ, S, D])
    cvec = din("cvec", [4, D])
    xs = din("xs", [T, D])
    ck = din("ck", [PAST, ATT])
    cv = din("cv", [PAST, ATT])
    clf = din("clf", [PAST, NH])
    cconv = din("cconv", [2, CONV])
    w_ada = din("w_ada", [D, 6 * D])
    b_ada = din("b_ada", [6 * D])
    g1 = din("g1", [D])
    g2 = din("g2", [D])
    w_in = din("w_in", [D, PROJ])
    b_f = din("b_f", [NH])
    w_conv = din("w_conv", [3, CONV])
    g_att = din("g_att", [ATT])
    g_conv = din("g_conv", [CONV])
    w_out = din("w_out", [D, D])
    w_up = din("w_up", [D, DFF])
    w_down = din("w_down", [DFF, D])
    w_adaf = din("w_adaf", [D, 2 * D])
    b_adaf = din("b_adaf", [2 * D])
    g_f = din("g_f", [D])

    yp = dout("yp", [NSEQ, S, D])
    kp = dout("kp", [NSEQ, S, ATT])
    vp = dout("vp", [NSEQ, S, ATT])
    lfp = dout("lfp", [NSEQ, S, NH])
    convp = dout("convp", [NSEQ, 2, CONV])
    ys = dout("ys", [T, D])
    ks = dout("ks", [T, ATT])
    vs = dout("vs", [T, ATT])
    lfs = dout("lfs", [T, NH])
    convs = dout("convs", [2, CONV])

    win_b = dscr("win_b", [D, PROJ], BF16)
    wout_b = dscr("wout_b", [D, D], BF16)
    wup_b = dscr("wup_b", [D, DFF], BF16)
    wdn_b = dscr("wdn_b", [DFF, D], BF16)

    import contextlib
    stack = contextlib.ExitStack()
    stack.__enter__()

    def sbuf(name, shape, dt):
        t = stack.enter_context(nc.sbuf_tensor(name, list(shape), dt))
        return Buf(t[tuple(slice(None) for _ in shape)], name, 0, shape, esize_of(dt))

    def view(parent_t, arena, off_bytes, shape, dt):
        es = esize_of(dt)
        n = 1
        for d in shape[1:]:
            n *= d
        nbytes = n * es
        assert off_bytes % 4 == 0 and nbytes % 4 == 0
        ap = parent_t[:, off_bytes // 4:(off_bytes + nbytes) // 4]
        if dt != F32:
            ap = ap.bitcast(dt)
        if len(shape) == 3:
            ap = ap.rearrange("p (a b) -> p a b", b=shape[2])
        elif len(shape) == 4:
            ap = ap.rearrange("p (a b c) -> p a b c", b=shape[2], c=shape[3])
        return Buf(ap, arena, off_bytes, shape, es)

    maxtiles = max(S // 128, (PAST // 128 + 1) if SAMPLE else 0)

    kT = sbuf("kT", [128, 4, maxtiles * 128], BF16)
    Vp = sbuf("Vp", [128, maxtiles, NH, 66], BF16)
    Vp2 = Buf(Vp.ap.rearrange("p t (h q) d -> p t h q d", q=2), "Vp", 0, [128, maxtiles, 4, 2, 66], 2)
    Ftok = sbuf("Ftok", [128, maxtiles, NH], F32)
    Eend = sbuf("Eend", [128, maxtiles + 1, NH], F32)
    kfac = sbuf("kfac", [128, maxtiles, NH], F32)
    xT = sbuf("xT", [128, 8, GW], F32)
    hT = sbuf("hT", [128, 8, GW], BF16)
    rstd = sbuf("rstd", [128, GW], F32)
    tmpA = sbuf("tmpA", [128, 2, GW], F32)
    sq = sbuf("sq", [128, 2, GW], BF16)
    wring = sbuf("wring", [128, 3, 8, 512], BF16)
    yst = sbuf("yst", [128, 2, D], F32)
    att0 = Buf(yst.ap.rearrange("p a (b c) -> p (a b) c", c=ATT), "yst", 0, [128, 4, ATT], 4)
    attn = sbuf("attn", [128, 4, ATT], BF16)
    vsraw = stack.enter_context(nc.sbuf_tensor("vsraw", [128, 520], F32))
    Vs = view(vsraw, "vsraw", 0, [128, 4, 4, 65], BF16)
    rbuf = view(vsraw, "vsraw", 0, [128, 2, GW], BF16)
    qpad = sbuf("qpad", [128, NH, GW], BF16)
    expD = sbuf("expD", [128, 4, maxtiles, NH], F32)
    small = sbuf("small", [128, 64], F32)
    lfst = sbuf("lfst", [128, 4, NH], F32)
    flx = sbuf("flx", [128, 4, NH], F32)
    identF = sbuf("identF", [128, 128], F32)
    identB = sbuf("identB", [128, 128], BF16)
    triF = sbuf("triF", [128, 128], F32)
    triB = sbuf("triB", [128, 128], BF16)
    selF = sbuf("selF", [128, 128], F32)
    onesF = sbuf("onesF", [128, 128], F32)
    onesB = sbuf("onesB", [128, 128], BF16)
    modc = sbuf("modc", [128, 3, 8, 8], F32)
    gcol = sbuf("gcol", [128, 5, 8], F32)
    wflb = sbuf("wflb", [128, 8, NH], BF16)
    wcv = sbuf("wcv", [128, 4, 3], F32)
    bfb = sbuf("bfb", [128, NH], F32)
    U1W = 33 * 1024
    u1 = stack.enter_context(nc.sbuf_tensor("u1", [128, U1W // 4], F32))
    aT = view(u1, "u1", 0, [128, 32, GW], BF16)
    modr = Buf(u1[0:4, 0:8 * D], "u1", 0, [4, 8 * D], 4)
    kst = view(u1, "u1", 0, [128, 4, ATT], F32)
    zT = view(u1, "u1", 0, [128, 4, GW], F32)
    vst = view(u1, "u1", 8192, [128, 4, ATT], F32)
    usb = view(u1, "u1", 16384, [128, GW], F32)
    cu = view(u1, "u1", 18432, [128, 4, GW + 2], F32)
    ycv = view(u1, "u1", 26656, [128, GW], F32)
    chist = sbuf("chist", [128, 4, 2], F32)
    u2 = stack.enter_context(nc.sbuf_tensor("u2", [128, 4 * D], F32))
    xin = view(u2, "u2", 0, [128, 4, D], F32)
    mT = view(u2, "u2", 0, [128, 8, GW], BF16)
    qT = view(u2, "u2", 8192, [128, 4, GW], BF16)
    Pr = view(u2, "u2", 12288, [128, 4, 512], BF16)

    psum_t = [stack.enter_context(nc.psum_tensor("ps%d" % i, [128, 512], F32)) for i in range(8)]

    def PS(bank, c0=0, c1=512, p0=0, p1=128):
        return Reg(psum_t[bank][p0:p1, c0:c1], "psum", bank * 2048 + c0 * 4, bank * 2048 + c1 * 4)

    def PSB(bank, c0, c1, p0=0, p1=128):
        ap = psum_t[bank][p0:p1, :].bitcast(BF16)[:, c0:c1]
        return Reg(ap, "psum", bank * 2048 + c0 * 2, bank * 2048 + c1 * 2)

    ring_state = {"i": 0}
    sc_cnt = {"p": 0, "v": 0, "y": 0, "acc": 0, "at": 0}

    def next_bank(nb=4):
        b = ring_state["i"] % nb
        ring_state["i"] += 1
        return b

    def DR(ap, name, lo=0, hi=1):
        return Reg(ap, "dram:" + name, lo, hi)

    def mm(out, lhsT, rhs, start=True, stop=True, extra_reads=()):
        add("pe", lambda e: e.matmul(out.ap, lhsT=lhsT.ap, rhs=rhs.ap, start=start, stop=stop,
                                     skip_group_check=True),
            reads=[lhsT, rhs] + list(extra_reads) + ([] if start else [out]), writes=[out])

    def tr(out, in_, ident):
        add("pe", lambda e: e.transpose(out.ap, in_.ap, ident.ap), reads=[in_, ident], writes=[out])

    def act(out, in_, func, bias=0.0, scale=1.0, eng="act"):
        rd = [in_]
        b = bias
        s = scale
        if isinstance(bias, Reg):
            rd.append(bias)
            b = bias.ap
        if isinstance(scale, Reg):
            rd.append(scale)
            s = scale.ap
        add("act", lambda e: e.activation(out.ap, in_.ap, func, bias=b, scale=s), reads=rd, writes=[out])

    def tt(eng, out, a, b, op):
        add(eng, lambda e: e.tensor_tensor(out.ap, a.ap, b.ap, op), reads=[a, b], writes=[out])

    def ts(eng, out, a, s1, op0, s2=None, op1=None):
        rd = [a]
        v1 = s1
        v2 = s2
        if isinstance(s1, Reg):
            rd.append(s1)
            v1 = s1.ap
        if isinstance(s2, Reg):
            rd.append(s2)
            v2 = s2.ap
        if op1 is None:
            add(eng, lambda e: e.tensor_scalar(out.ap, a.ap, v1, None, op0), reads=rd, writes=[out])
        else:
            add(eng, lambda e: e.tensor_scalar(out.ap, a.ap, v1, v2, op0, op1), reads=rd, writes=[out])

    def stt(out, a, s, b, op0, op1, accum=None):
        rd = [a, b]
        v = s
        if isinstance(s, Reg):
            rd.append(s)
            v = s.ap
        wr = [out]
        if accum is not None:
            wr.append(accum)
            add("dve", lambda e: e.scalar_tensor_tensor(out.ap, a.ap, v, b.ap, op0, op1, accum_out=accum.ap),
                reads=rd, writes=wr)
        else:
            add("dve", lambda e: e.scalar_tensor_tensor(out.ap, a.ap, v, b.ap, op0, op1), reads=rd, writes=wr)

    def cp(eng, out, in_):
        if eng == "act":
            add("act", lambda e: e.copy(out.ap, in_.ap), reads=[in_], writes=[out])
        else:
            add(eng, lambda e: e.tensor_copy(out.ap, in_.ap), reads=[in_], writes=[out])

    def dma(eng, out, in_):
        add(eng, lambda e: e.dma_start(out=out.ap, in_=in_.ap), reads=[in_], writes=[out], dma=True)

    def memset(eng, out, v):
        add(eng, lambda e: e.memset(out.ap, v), writes=[out])

    def bc(reg, shape):
        return Reg(reg.ap.to_broadcast(list(shape)), reg.arena, reg.lo, reg.hi)

    memset("pool", onesF.all(), 1.0)
    add("pool", lambda e: e.affine_select(identF.all().ap, onesF.all().ap, [[-1, 128]], ALU.is_equal, 0.0,
                                          base=0, channel_multiplier=1),
        reads=[onesF.all()], writes=[identF.all()])
    add("pool", lambda e: e.affine_select(triF.all().ap, onesF.all().ap, [[1, 128]], ALU.is_ge, 0.0,
                                          base=0, channel_multiplier=-1),
        reads=[onesF.all()], writes=[triF.all()])
    add("pool", lambda e: e.affine_select(selF.all().ap, onesF.all().ap, [[0, 128]], ALU.is_equal, 0.0,
                                          base=-127, channel_multiplier=1),
        reads=[onesF.all()], writes=[selF.all()])
    cp("pool", identB.all(), identF.all())
    cp("pool", triB.all(), triF.all())
    cp("pool", onesB.all(), onesF.all())
    memset("pool", Eend[:, 0, :], 0.0)
    memset("pool", qpad.all(), 0.0)

    def cast_rows(dst, src, name, rows, step):
        for r0 in range(0, rows, step):
            dma("pool", DR(dst[r0:r0 + step, :], name, r0 // 256, (r0 + step) // 256), DR(src[r0:r0 + step, :], name + "_src"))

    import os
    SKIP = set(os.environ.get("KSKIP", "").split(","))
    def colload(dst, src_ap, name):
        add("sp", lambda e: e.dma_start(out=dst.ap, in_=src_ap.rearrange("(c p) -> p c", p=128),
                                        allow_slow_non_contiguous=True),
            reads=[DR(src_ap, name)], writes=[dst], dma=True)

    colload(gcol[:, 0, :], g1, "g1")
    colload(gcol[:, 1, :], g2, "g2")
    colload(gcol[:, 2, :], g_f, "g_f")
    add("sp", lambda e: e.dma_start(out=gcol[:, 3, 0:4].ap, in_=g_att.rearrange("(c p) -> p c", p=128),
                                    allow_slow_non_contiguous=True),
        reads=[DR(g_att, "g_att")], writes=[gcol[:, 3, 0:4]], dma=True)
    add("sp", lambda e: e.dma_start(out=gcol[:, 3, 4:8].ap, in_=g_conv.rearrange("(c p) -> p c", p=128),
                                    allow_slow_non_contiguous=True),
        reads=[DR(g_conv, "g_conv")], writes=[gcol[:, 3, 4:8]], dma=True)
    for j_ in range(3):
        add("sp", lambda e, j_=j_: e.dma_start(out=wcv[:, :, j_].ap, in_=w_conv[j_].rearrange("(c p) -> p c", p=128),
                                               allow_slow_non_contiguous=True),
            reads=[DR(w_conv, "w_conv")], writes=[wcv.all()], dma=True)
    add("sp", lambda e: e.dma_start(out=bfb.all().ap, in_=b_f.partition_broadcast(128)),
        reads=[DR(b_f, "b_f")], writes=[bfb.all()], dma=True)
    add("pool", lambda e: e.dma_start(out=wflb.all().ap,
                                      in_=w_in[:, 3 * ATT:3 * ATT + NH].rearrange("(c p) n -> p c n", p=128)),
        reads=[DR(w_in, "w_in")], writes=[wflb.all()], dma=True)

    cT = sbuf("cT", [128, 8, 4], F32)
    sT = sbuf("sT", [128, 8, 4], BF16)
    for s_ in range(4):
        add("sp", lambda e, s_=s_: e.dma_start(out=cT[:, :, s_].ap, in_=cvec[s_].rearrange("(c p) -> p c", p=128),
                                               allow_slow_non_contiguous=True),
            reads=[DR(cvec, "cvec")], writes=[cT.all()], dma=True)
    ctmp = sbuf("ctmp", [128, 8, 4], F32)
    act(ctmp.all(), cT.all(), AF.Exp, scale=-1.0)
    ts("dve", ctmp.all(), ctmp.all(), 1.0, ALU.add)
    add("dve", lambda e: e.reciprocal(ctmp.all().ap, ctmp.all().ap), reads=[ctmp.all()], writes=[ctmp.all()])
    tt("dve", sT.all(), cT.all(), ctmp.all(), ALU.mult)
    brow = Buf(u2[0:4, 0:1024].rearrange("p (a b) -> p a b", b=512), "u2", 0, [4, 2, 512], 4)
    wslot = {"i": 0}

    def wload(src_ap, name, eng="sp", key=0):
        s = wslot["i"] % 3
        wslot["i"] += 1
        dst = wring[:, s, :, :]
        add(eng, lambda e: e.dma_start(out=dst.ap, in_=src_ap), reads=[DR(src_ap, name, key, key + 1)],
            writes=[dst], dma=True)
        return s

    for blk in range(0 if "ada" in SKIP else 16):
        if blk < 12:
            src = w_ada[:, blk * 512:(blk + 1) * 512]
        else:
            src = w_adaf[:, (blk - 12) * 512:(blk - 11) * 512]
        s = wload(src.rearrange("(c p) n -> p c n", p=128), "w_ada_src", eng="pool")
        b = next_bank()
        for c in range(8):
            mm(PS(b, 0, 512, 0, 4), sT[:, c, :], wring[:, s, c, :], start=(c == 0), stop=(c == 7))
        bsrc = b_ada[blk * 512:(blk + 1) * 512] if blk < 12 else b_adaf[(blk - 12) * 512:(blk - 11) * 512]
        add("sp", lambda e, bsrc=bsrc, blk=blk: e.dma_start(out=brow[0:4, blk % 2, :].ap, in_=bsrc.partition_broadcast(4)),
            reads=[DR(b_ada, "b_ada")], writes=[brow[0:4, blk % 2, :]], dma=True)
        tt("dve", modr[0:4, blk * 512:(blk + 1) * 512], PS(b, 0, 512, 0, 4), brow[0:4, blk % 2, :], ALU.add)
    modcol = view(u2, "u2", 4096, [128, 8, 8, 4], F32)
    for v in range(8):
        b = next_bank()
        for c in range(8):
            tr(PS(b, c * 4, c * 4 + 4), modr[0:4, v * D + c * 128: v * D + (c + 1) * 128], identF[0:4, 0:4])
        add("dve", lambda e, b=b, v=v: e.tensor_copy(modcol[:, v, :, :].ap,
                                                     PS(b, 0, 32).ap.rearrange("p (c s) -> p c s", s=4)),
            reads=[PS(b, 0, 32)], writes=[modcol[:, v, :, :]])
    for s_ in range(3):
        for (dst, scv, gi) in ((0, 1, 0), (3, 4, 1), (6, 7, 2)):
            stt(modc[:, s_, dst, :], modcol[:, scv, :, s_], 1.0, gcol[:, gi, :], ALU.add, ALU.mult)
        for (dst, srcv) in ((1, 0), (2, 2), (4, 3), (5, 5), (7, 6)):
            cp("dve", modc[:, s_, dst, :], modcol[:, srcv, :, s_])

    if "cast" not in SKIP:
        cast_rows(win_b, w_in, "win_b", D, 256)
        cast_rows(wout_b, w_out, "wout_b", D, 512)
        cast_rows(wup_b, w_up, "wup_b", D, 256)
        cast_rows(wdn_b, w_down, "wdn_b", DFF, 1024)

    def wblock(w_ap, name, r0, c0, key):
        src = w_ap[r0:r0 + 1024, c0:c0 + 512].rearrange("(c p) n -> p c n", p=128)
        s = wslot["i"] % 3
        wslot["i"] += 1
        dst = wring[:, s, :, :]
        add("sp", lambda e: e.dma_start(out=dst.ap, in_=src), reads=[DR(w_ap, name, 0, 16)],
            writes=[dst], dma=True)
        return s

    def rms_bc(srcs, nfeat, W, out_rstd):
        b = next_bank()
        n = len(srcs)
        for i, s_ in enumerate(srcs):
            sl = sq[:, i % 2, 0:W]
            tt("pool" if i % 2 == 0 else "dve", sl, s_, s_, ALU.mult)
            mm(PS(b, 0, W), onesB.all(), sl, start=(i == 0), stop=(i == n - 1))
        act(out_rstd, PS(b, 0, W), AF.Ln, bias=small_eps, scale=1.0 / nfeat)
        act(out_rstd, out_rstd, AF.Exp, scale=-0.5)

    def rms_feed(b, i, n, src, W):
        sl = sq[:, i % 2, 0:W]
        tt("pool" if i % 2 == 0 else "dve", sl, src, src, ALU.mult)
        mm(PS(b, 0, W), onesB.all(), sl, start=(i == 0), stop=(i == n - 1))

    def rms_finish(b, nfeat, W, out_rstd):
        act(out_rstd, PS(b, 0, W), AF.Ln, bias=small_eps, scale=1.0 / nfeat)
        act(out_rstd, out_rstd, AF.Exp, scale=-0.5)

    memset("pool", small[:, 0:1], EPS)
    small_eps = small[:, 0:1]

    xstate = {}

    def load_x(x_src, t0, ntl, TW):
        for tl in range(ntl):
            dma("act", xin[0:TW, tl, :], DR(x_src[t0 + tl * 128: t0 + tl * 128 + TW, :], "x_src"))

    def run_sequence(si, x_src, ntok, hist_tiles, outs, is_sample, next_x=None):
        y_dst, k_dst, v_dst, lf_dst, conv_dst = outs
        ngroups = (ntok + GW - 1) // GW
        for g in range(ngroups):
            W = min(GW, ntok - g * GW)
            ntl = (W + 127) // 128
            TW = min(128, W)
            t0 = g * GW
            if xstate.get("loaded") != (si, g):
                load_x(x_src, t0, ntl, TW)
            xstate["loaded"] = None
            for c in range(8):
                b = next_bank()
                for tl in range(ntl):
                    tr(PS(b, tl * 128, tl * 128 + TW), xin[0:TW, tl, c * 128:(c + 1) * 128], identF[0:TW, 0:TW])
                cp("act", xT[:, c, 0:W], PS(b, 0, W))
                if c >= 2:
                    rms_feed(7, c - 2, 8, xT[:, c - 2, 0:W], W)
            rms_feed(7, 6, 8, xT[:, 6, 0:W], W)
            rms_feed(7, 7, 8, xT[:, 7, 0:W], W)
            rms_finish(7, D, W, rstd[:, 0:W])
            for c in range(8):
                tt("dve", tmpA[:, c % 2, 0:W], xT[:, c, 0:W], rstd[:, 0:W], ALU.mult)
                act(hT[:, c, 0:W], tmpA[:, c % 2, 0:W], AF.Identity, bias=modc[:, si, 1, c:c + 1], scale=modc[:, si, 0, c:c + 1])
            if cfg.get("STAGE", 9) <= 0.2:
                continue
            bfl = next_bank()
            for tl in range(ntl):
                for c in range(8):
                    mm(PS(bfl, tl * NH, tl * NH + NH, 0, TW), hT[:, c, tl * 128: tl * 128 + TW], wflb[:, c, :],
                       start=(c == 0), stop=(c == 7))
            for tl in range(ntl):
                tt("dve", flx[0:TW, tl, :], PS(bfl, tl * NH, tl * NH + NH, 0, TW), bfb[0:TW, :], ALU.add)
            act(flx[0:TW, 0:ntl, :], flx[0:TW, 0:ntl, :], AF.Exp, scale=-1.0)
            act(flx[0:TW, 0:ntl, :], flx[0:TW, 0:ntl, :], AF.Ln, bias=1.0)
            if TW < 128:
                memset("dve", lfst.all(), 0.0)
            ts("dve", lfst[0:TW, 0:ntl, :], flx[0:TW, 0:ntl, :], -1.0, ALU.mult)
            for tl in range(ntl):
                dma("act", DR(lf_dst[t0 + tl * 128: t0 + tl * 128 + TW, :], "lf_dst", t0 + tl * 128, t0 + tl * 128 + TW),
                    lfst[0:TW, tl, :])
            bF = next_bank()
            mm(PS(bF, 0, ntl * NH), triF.all(), lfst[:, 0:ntl, :])
            mm(PS(bF, 64, 64 + ntl * NH), onesF.all(), lfst[:, 0:ntl, :])
            for tl in range(ntl):
                j = hist_tiles + (t0 // 128) + tl
                tt("dve", Ftok[:, j, :], PS(bF, tl * NH, tl * NH + NH), Eend[:, j, :], ALU.add)
                tt("dve", Eend[:, j + 1, :], PS(bF, 64 + tl * NH, 64 + tl * NH + NH), Eend[:, j, :], ALU.add)
                tt("dve", kfac[:, j, :], Eend[:, j + 1, :], Ftok[:, j, :], ALU.subtract)
            j0 = hist_tiles + t0 // 128
            act(kfac[:, j0:j0 + ntl, :], kfac[:, j0:j0 + ntl, :], AF.Exp)
            if cfg.get("STAGE", 9) <= 0.3:
                continue
            s = wblock(win_b, "win_b", 0, 0, 0)
            for p in range(4):
                b = next_bank()
                for c in range(8):
                    mm(PS(b, 0, W), wring[:, s, c, p * 128:(p + 1) * 128], hT[:, c, 0:W], start=(c == 0), stop=(c == 7))
                cp("act", qpad[0:64, 2 * p, 0:W], PS(b, 0, W, 0, 64))
                cp("act", qpad[64:128, 2 * p + 1, 0:W], PS(b, 0, W, 64, 128))
            if cfg.get("STAGE", 9) <= 0.4:
                continue
            s = wblock(win_b, "win_b", 0, ATT, 1)
            for tl in range(0 if "kpart" in SKIP else ntl):
                j = j0 + tl
                b = next_bank()
                for c in range(8):
                    mm(PS(b, 0, 512, 0, TW), hT[:, c, tl * 128: tl * 128 + TW], wring[:, s, c, :], start=(c == 0), stop=(c == 7))
                cp("act", kst[0:TW, tl, :], PS(b, 0, 512, 0, TW))
                dma("act", DR(k_dst[t0 + tl * 128: t0 + tl * 128 + TW, :], "k_dst", t0 + tl * 128, t0 + tl * 128 + TW),
                    kst[0:TW, tl, :])
                b2 = next_bank()
                for p in range(4):
                    tr(PS(b2, p * 128, p * 128 + TW), kst[0:TW, tl, p * 128:(p + 1) * 128], identF[0:TW, 0:TW])
                add("dve", lambda e, b2=b2, j=j, TW=TW: e.tensor_copy(
                    kT[:, :, j * 128: j * 128 + TW].ap,
                    psum_t[b2][:, :].rearrange("p (a t) -> p a t", t=128)[:, :, 0:TW]),
                    reads=[PS(b2, 0, 512)], writes=[kT[:, p_, j * 128: j * 128 + TW] for p_ in range(4)])
            if cfg.get("STAGE", 9) <= 0.5:
                continue
            s = wblock(win_b, "win_b", 0, 2 * ATT, 2)
            for tl in range(ntl):
                j = j0 + tl
                b = next_bank()
                for c in range(8):
                    mm(PS(b, 0, 512, 0, TW), hT[:, c, tl * 128: tl * 128 + TW], wring[:, s, c, :], start=(c == 0), stop=(c == 7))
                cp("act", vst[0:TW, tl, :], PS(b, 0, 512, 0, TW))
                if "vdma" not in SKIP:
                    dma("act", DR(v_dst[t0 + tl * 128: t0 + tl * 128 + TW, :], "v_dst", t0 + tl * 128, t0 + tl * 128 + TW),
                        vst[0:TW, tl, :])
                if "vops" in SKIP:
                    continue
                if TW < 128:
                    memset("pool", Vp[:, j, :, :], 0.0)
                if "vtt" in SKIP:
                    for h_ in range(NH):
                        ts("dve", Vp[0:TW, j, h_, 0:64], vst[0:TW, tl, h_ * 64:(h_ + 1) * 64], kfac[0:TW, j, h_:h_ + 1], ALU.mult)
                else:
                    add("dve", lambda e, tl=tl, j=j, TW=TW: e.tensor_tensor(
                        Vp[0:TW, j, :, 0:64].ap,
                        vst[0:TW, tl, :].ap.rearrange("p (h d) -> p h d", d=64),
                        kfac[0:TW, j, :].ap.unsqueeze(2).to_broadcast([TW, NH, 64]), ALU.mult),
                        reads=[vst[0:TW, tl, :], kfac[0:TW, j, :]], writes=[Vp[0:TW, j, :, :]])
                if "vp1" not in SKIP:
                    add("pool", lambda e, j=j, TW=TW: e.tensor_copy(Vp[0:TW, j, :, 64:65].ap, kfac[0:TW, j, :].ap.unsqueeze(2)),
                        reads=[kfac[0:TW, j, :]], writes=[Vp[0:TW, j, :, :]])
            if cfg.get("STAGE", 9) <= 1:
                continue
            LOOK = 2
            NQ = ntl
            TWq = TW
            for ii in range(NQ):
                i = j0 + ii
                add("dve", lambda e, i=i, ii=ii: e.tensor_tensor(
                    expD[:, ii, 0:i + 1, :].ap, Eend[:, i + 1:i + 2, :].ap.to_broadcast([128, i + 1, NH]),
                    Eend[:, 1:i + 2, :].ap, ALU.subtract),
                    reads=[Eend[:, i + 1, :], Eend[:, 1:i + 2, :]], writes=[expD[:, ii, 0:i + 1, :]])
                act(expD[:, ii, 0:i + 1, :], expD[:, ii, 0:i + 1, :], AF.Exp)
            accb = [4, 5, 6, 7]
            nk = j0 + NQ
            for hh in range(2):
                steps = [(hl, j) for hl in range(4) for j in range(nk)]
                info = {}

                def front(sidx, hh=hh, steps=steps, info=info):
                    hl, j = steps[sidx]
                    h = 2 * hl + hh
                    p = h // 2
                    r0 = (h % 2) * 64
                    iim = max(0, j - j0)
                    TWk = TWq if j == nk - 1 else 128
                    c0_ = iim * 128
                    bS = next_bank()
                    mm(PS(bS, c0_, W, 0, TWk), kT[:, p, j * 128: j * 128 + TWk], qpad[:, h, c0_:W])
                    slot = sc_cnt["p"] % 4
                    sc_cnt["p"] += 1
                    act(Pr[0:TWk, slot, c0_:W], PS(bS, c0_, W, 0, TWk), AF.Exp, scale=ATT_SCALE)
                    if j >= j0:
                        tt("pool", Pr[0:TWk, slot, c0_:c0_ + TWq], Pr[0:TWk, slot, c0_:c0_ + TWq], triB[0:TWk, 0:TWq], ALU.mult)
                    vslot = sc_cnt["v"] % 4
                    sc_cnt["v"] += 1
                    n_ = NQ - iim
                    veng = "dve"
                    add(veng, lambda e, vslot=vslot, j=j, h=h, iim=iim, n_=n_: e.tensor_tensor(
                        Vs[:, vslot, iim:NQ, :].ap,
                        Vp[:, j, h, 0:65].ap.unsqueeze(1).to_broadcast([128, n_, 65]),
                        expD[:, iim:NQ, j, h].ap.unsqueeze(2).to_broadcast([128, n_, 65]), ALU.mult),
                        reads=[Vp[:, j, h, :], expD[:, iim:NQ, j, h]], writes=[Vs[:, vslot, iim:NQ, :]])
                    info[sidx] = (slot, vslot, TWk, iim)

                def back(sidx, hh=hh, steps=steps, info=info):
                    hl, j = steps[sidx]
                    slot, vslot, TWk, iim = info[sidx]
                    for ii in range(iim, NQ):
                        mm(PS(accb[ii], hl * 65, hl * 65 + 65, 0, TWq), Pr[0:TWk, slot, ii * 128: ii * 128 + TWq],
                           Vs[0:TWk, vslot, ii, 0:65], start=(j == 0 and hl == 0), stop=(j == j0 + ii and hl == 3))

                if is_sample and NQ == 1 and TWq <= 16:
                    Vsb = Buf(Vs.ap.rearrange("p a b d -> p (a b) d"), "vsraw", 0, [128, 16, 65], 2)
                    for hl in range(4):
                        h = 2 * hl + hh
                        p = h // 2
                        chunks = [list(range(a, min(a + 16, j0))) for a in range(0, j0, 16)] + [[j0]]
                        for ch in chunks:
                            own = (ch[0] == j0)
                            TWk = TWq if own else 128
                            n_ = len(ch)
                            bS = next_bank()
                            for jj, j in enumerate(ch):
                                mm(PS(bS, jj * 16, jj * 16 + TWq, 0, TWk), kT[:, p, j * 128: j * 128 + TWk], qpad[:, h, 0:TWq])
                            slot = sc_cnt["p"] % 4
                            sc_cnt["p"] += 1
                            if TWq == 16:
                                act(Pr[0:TWk, slot, 0:n_ * 16], PS(bS, 0, n_ * 16, 0, TWk), AF.Exp, scale=ATT_SCALE)
                            else:
                                add("act", lambda e, bS=bS, slot=slot, TWk=TWk, n_=n_: e.activation(
                                    Pr[0:TWk, slot, 0:n_ * 16].ap.rearrange("p (a t) -> p a t", t=16)[:, :, 0:TWq],
                                    psum_t[bS][0:TWk, 0:n_ * 16].rearrange("p (a t) -> p a t", t=16)[:, :, 0:TWq],
                                    AF.Exp, scale=ATT_SCALE),
                                    reads=[PS(bS, 0, n_ * 16, 0, TWk)], writes=[Pr[0:TWk, slot, 0:n_ * 16]])
                            if own:
                                tt("pool", Pr[0:TWk, slot, 0:TWq], Pr[0:TWk, slot, 0:TWq], triB[0:TWk, 0:TWq], ALU.mult)
                            ja, jb = ch[0], ch[-1] + 1
                            add("dve", lambda e, ja=ja, jb=jb, h=h, n_=n_: e.tensor_tensor(
                                Vsb[:, 0:n_, :].ap, Vp[:, ja:jb, h, 0:65].ap,
                                expD[:, 0, ja:jb, h].ap.unsqueeze(2).to_broadcast([128, n_, 65]), ALU.mult),
                                reads=[Vp[:, ja:jb, h, :], expD[:, 0, ja:jb, h]], writes=[Vsb[:, 0:n_, :]])
                            for jj, j in enumerate(ch):
                                mm(PS(accb[0], hl * 65, hl * 65 + 65, 0, TWq), Pr[0:TWk, slot, jj * 16: jj * 16 + TWq],
                                   Vsb[0:TWk, jj, 0:65], start=(j == 0 and hl == 0), stop=(j == j0 and hl == 3))
                else:
                    n_st = len(steps)
                    for sidx in range(n_st + LOOK):
                        if sidx < n_st:
                            front(sidx)
                        if sidx - LOOK >= 0:
                            back(sidx - LOOK)
                for ii in range(NQ):
                    cb = 8 + (ii * 2 + hh) * 4
                    add("dve", lambda e, cb=cb, ab=accb[ii]: e.reciprocal(
                        small[0:TWq, cb:cb + 4].ap,
                        psum_t[ab][0:TWq, 0:260].rearrange("p (h d) -> p h d", d=65)[:, :, 64]),
                        reads=[PS(accb[ii], 0, 260, 0, TWq)], writes=[small[0:TWq, cb:cb + 4]])
                    add("dve", lambda e, cb=cb, ab=accb[ii], ii=ii, hh=hh: e.tensor_tensor(
                        att0[0:TWq, ii, :].ap.rearrange("p (h q d) -> p h q d", q=2, d=64)[:, :, hh, :],
                        psum_t[ab][0:TWq, 0:260].rearrange("p (h d) -> p h d", d=65)[:, :, 0:64],
                        small[0:TWq, cb:cb + 4].ap.unsqueeze(2).to_broadcast([TWq, 4, 64]), ALU.mult),
                        reads=[PS(accb[ii], 0, 260, 0, TWq), small[0:TWq, cb:cb + 4]],
                        writes=[att0[0:TWq, ii, :]])
            for ii in range(NQ):
                stt(tmpA[0:TWq, ii % 2, 0:ATT], att0[0:TWq, ii, :], 1.0, att0[0:TWq, ii, :], ALU.mult, ALU.mult,
                    accum=small[0:TWq, 40 + ii:41 + ii])
                act(small[0:TWq, 44 + ii:45 + ii], small[0:TWq, 40 + ii:41 + ii], AF.Ln, bias=small[0:TWq, 0:1], scale=1.0 / ATT)
                act(small[0:TWq, 48 + ii:49 + ii], small[0:TWq, 44 + ii:45 + ii], AF.Exp, scale=-0.5)
                ts("dve", attn[0:TWq, ii, :], att0[0:TWq, ii, :], small[0:TWq, 48 + ii:49 + ii], ALU.mult)
            sC = wblock(win_b, "win_b", 0, 3 * ATT + NH + 2 * CONV, 5)
            sB = wblock(win_b, "win_b", 0, 3 * ATT + NH + CONV, 4)
            sA = wblock(win_b, "win_b", 0, 3 * ATT + NH, 3)
            for cc in range(4):
                b = next_bank()
                for c in range(8):
                    mm(PS(b, 0, W), wring[:, sC, c, cc * 128:(cc + 1) * 128], hT[:, c, 0:W], start=(c == 0), stop=(c == 7))
                cp("act", usb[:, 0:W], PS(b, 0, W))
                b = next_bank()
                for c in range(8):
                    mm(PS(b, 0, W), wring[:, sB, c, cc * 128:(cc + 1) * 128], hT[:, c, 0:W], start=(c == 0), stop=(c == 7))
                if g == 0:
                    if is_sample:
                        for j_ in range(2):
                            add("sp", lambda e, j_=j_, cc=cc: e.dma_start(
                                out=cu[:, cc, j_:j_ + 1].ap,
                                in_=cconv[j_, cc * 128:(cc + 1) * 128].rearrange("(p o) -> p o", o=1),
                                allow_slow_non_contiguous=True),
                                reads=[DR(cconv, "cconv")], writes=[cu[:, cc, j_:j_ + 1]], dma=True)
                    else:
                        memset("pool", cu[:, cc, 0:2], 0.0)
                else:
                    cp("pool", cu[:, cc, 0:2], chist[:, cc, :])
                tt("dve", cu[:, cc, 2:2 + W], PS(b, 0, W), usb[:, 0:W], ALU.mult)
                cp("pool", chist[:, cc, :], cu[:, cc, W:W + 2])
                ts("dve", ycv[:, 0:W], cu[:, cc, 0:W], wcv[:, cc, 0:1], ALU.mult)
                stt(ycv[:, 0:W], cu[:, cc, 1:1 + W], wcv[:, cc, 1:2], ycv[:, 0:W], ALU.mult, ALU.add)
                stt(ycv[:, 0:W], cu[:, cc, 2:2 + W], wcv[:, cc, 2:3], ycv[:, 0:W], ALU.mult, ALU.add)
                b = next_bank()
                for c in range(8):
                    mm(PS(b, 0, W), wring[:, sA, c, cc * 128:(cc + 1) * 128], hT[:, c, 0:W], start=(c == 0), stop=(c == 7))
                tt("dve", zT[:, cc, 0:W], PS(b, 0, W), ycv[:, 0:W], ALU.mult)
            rms_bc([zT[:, cc, 0:W] for cc in range(4)], CONV, W, rstd[:, 0:W])
            for cc in range(4):
                tt("dve", tmpA[:, cc % 2, 0:W], zT[:, cc, 0:W], rstd[:, 0:W], ALU.mult)
                act(mT[:, 4 + cc, 0:W], tmpA[:, cc % 2, 0:W], AF.Identity, scale=gcol[:, 3, 4 + cc:5 + cc])
            if g == ngroups - 1:
                for j_ in range(2):
                    for cc in range(4):
                        add("sp", lambda e, j_=j_, cc=cc: e.dma_start(
                            out=conv_dst[j_, cc * 128:(cc + 1) * 128].rearrange("(p o) -> p o", o=1),
                            in_=chist[:, cc, j_:j_ + 1].ap, allow_slow_non_contiguous=True),
                            reads=[chist[:, cc, j_:j_ + 1]], writes=[DR(conv_dst, "conv_dst", j_ * 4 + cc, j_ * 4 + cc + 1)], dma=True)
            for ii in range(NQ):
                bT = next_bank()
                for c in range(4):
                    tr(PSB(bT, c * 128, c * 128 + TWq), attn[0:TWq, ii, c * 128:(c + 1) * 128], identB[0:TWq, 0:TWq])
                for c in range(4):
                    ts("dve", mT[:, c, ii * 128: ii * 128 + TWq], PSB(bT, c * 128, c * 128 + TWq),
                       gcol[:, 3, c:c + 1], ALU.mult)
            if cfg.get("STAGE", 9) <= 2:
                continue
            so = [wblock(wout_b, "wout_b", 0, 0, 0), wblock(wout_b, "wout_b", 0, 512, 1)]
            for cc in range(8):
                b = next_bank()
                for c in range(8):
                    mm(PS(b, 0, W), wring[:, so[cc // 4], c, (cc % 4) * 128:(cc % 4 + 1) * 128], mT[:, c, 0:W],
                       start=(c == 0), stop=(c == 7))
                stt(xT[:, cc, 0:W], PS(b, 0, W), modc[:, si, 2, cc:cc + 1], xT[:, cc, 0:W], ALU.mult, ALU.add)
                if cc >= 2:
                    rms_feed(7, cc - 2, 8, xT[:, cc - 2, 0:W], W)
            if g + 1 < ngroups:
                W2 = min(GW, ntok - (g + 1) * GW)
                load_x(x_src, (g + 1) * GW, (W2 + 127) // 128, min(128, W2))
                xstate["loaded"] = (si, g + 1)
            elif next_x is not None:
                nsi, nsrc, nntok = next_x
                W2 = min(GW, nntok)
                load_x(nsrc, 0, (W2 + 127) // 128, min(128, W2))
                xstate["loaded"] = (nsi, 0)
            rms_feed(7, 6, 8, xT[:, 6, 0:W], W)
            rms_feed(7, 7, 8, xT[:, 7, 0:W], W)
            rms_finish(7, D, W, rstd[:, 0:W])
            for c in range(8):
                tt("dve", tmpA[:, c % 2, 0:W], xT[:, c, 0:W], rstd[:, 0:W], ALU.mult)
                act(hT[:, c, 0:W], tmpA[:, c % 2, 0:W], AF.Identity, bias=modc[:, si, 4, c:c + 1], scale=modc[:, si, 3, c:c + 1])
            for blk in range(8):
                s = wblock(wup_b, "wup_b", 0, blk * 512, blk)
                for q in range(4):
                    ffc = blk * 4 + q
                    b = next_bank()
                    for c in range(8):
                        mm(PS(b, 0, W), wring[:, s, c, q * 128:(q + 1) * 128], hT[:, c, 0:W], start=(c == 0), stop=(c == 7))
                    ts("dve", rbuf[:, ffc % 2, 0:W], PS(b, 0, W), 0.0, ALU.max)
                    tt("pool", aT[:, ffc, 0:W], rbuf[:, ffc % 2, 0:W], rbuf[:, ffc % 2, 0:W], ALU.mult)
            for half in range(2):
                banks = [4, 5, 6, 7] if half == 0 else [0, 1, 2, 3]
                for oc in range(4):
                    s = wblock(wdn_b, "wdn_b", oc * 1024, half * 512, half * 4 + oc)
                    if half == 1 and oc >= 1:
                        for c_ in ((0, 1), (2,), (3,))[oc - 1]:
                            rms_feed(4, c_, 8, xT[:, c_, 0:W], W)
                    for cc in range(4):
                        for fl_ in range(8):
                            mm(PS(banks[cc], 0, W), wring[:, s, fl_, cc * 128:(cc + 1) * 128], aT[:, oc * 8 + fl_, 0:W],
                               start=(oc == 0 and fl_ == 0), stop=(oc == 3 and fl_ == 7))
                for cc in range(4):
                    c = half * 4 + cc
                    stt(xT[:, c, 0:W], PS(banks[cc], 0, W), modc[:, si, 5, c:c + 1], xT[:, c, 0:W], ALU.mult, ALU.add)
            for c_ in range(4, 8):
                rms_feed(4, c_, 8, xT[:, c_, 0:W], W)
            rms_finish(4, D, W, rstd[:, 0:W])
            for c in range(8):
                tt("dve", tmpA[:, c % 2, 0:W], xT[:, c, 0:W], rstd[:, 0:W], ALU.mult)
                act(xT[:, c, 0:W], tmpA[:, c % 2, 0:W], AF.Identity, bias=modc[:, si, 7, c:c + 1], scale=modc[:, si, 6, c:c + 1])
            for tl in range(ntl):
                ysl = sc_cnt["y"] % 2
                sc_cnt["y"] += 1
                for half in range(2):
                    b = next_bank()
                    for q in range(4):
                        c = half * 4 + q
                        tr(PS(b, q * 128, (q + 1) * 128, 0, TW), xT[:, c, tl * 128: tl * 128 + TW], identF.all())
                    cp("act", yst[0:TW, ysl, half * 512:(half + 1) * 512], PS(b, 0, 512, 0, TW))
                dma("act", DR(y_dst[t0 + tl * 128: t0 + tl * 128 + TW, :], "y_dst", t0 + tl * 128, t0 + tl * 128 + TW),
                    yst[0:TW, ysl, :])
    for s_ in range(NSEQ if cfg.get("STAGE", 9) >= 1 else 0):
        nx = (s_ + 1, xp[s_ + 1], S) if s_ + 1 < NSEQ else None
        run_sequence(s_, xp[s_], S, 0, (yp[s_], kp[s_], vp[s_], lfp[s_], convp[s_]), False, next_x=nx)

    if SAMPLE:
        PT = PAST // 128
        lfc = view(u1, "u1", 30720, [128, PT, NH], F32)
        add("sp", lambda e: e.dma_start(out=lfc.all().ap, in_=clf.rearrange("(t p) h -> p t h", p=128)),
            reads=[DR(clf, "clf")], writes=[lfc.all()], dma=True)
        bF = next_bank()
        mm(PS(bF, 0, PT * NH), triF.all(), lfc.all())
        bG = next_bank()
        mm(PS(bG, 0, PT * NH), onesF.all(), lfc.all())
        for j in range(PT):
            tt("dve", Eend[:, j + 1, :], PS(bG, j * NH, (j + 1) * NH), Eend[:, j, :], ALU.add)
        add("dve", lambda e: e.tensor_tensor(Ftok[:, 0:PT, :].ap,
                                             psum_t[bF][:, 0:PT * NH].rearrange("p (t h) -> p t h", h=NH),
                                             Eend[:, 0:PT, :].ap, ALU.add),
            reads=[PS(bF, 0, PT * NH), Eend[:, 0:PT, :]], writes=[Ftok[:, 0:PT, :]])
        tt("dve", kfac[:, 0:PT, :], Eend[:, 1:PT + 1, :], Ftok[:, 0:PT, :], ALU.subtract)
        act(kfac[:, 0:PT, :], kfac[:, 0:PT, :], AF.Exp)
        for j in range(PT):
            sl = j % 4
            dma("sp", kst[:, sl, :], DR(ck[j * 128:(j + 1) * 128, :], "ck"))
            dma("sp", vst[:, sl, :], DR(cv[j * 128:(j + 1) * 128, :], "cv"))
            b2 = next_bank()
            for p in range(4):
                tr(PS(b2, p * 128, (p + 1) * 128), kst[:, sl, p * 128:(p + 1) * 128], identF.all())
            add("dve", lambda e, b2=b2, j=j: e.tensor_copy(
                kT[:, :, j * 128:(j + 1) * 128].ap, psum_t[b2][:, :].rearrange("p (a t) -> p a t", t=128)),
                reads=[PS(b2, 0, 512)], writes=[kT[:, p_, j * 128:(j + 1) * 128] for p_ in range(4)])
            add("pool", lambda e, sl=sl, j=j: e.tensor_tensor(
                Vp[:, j, :, 0:64].ap, vst[:, sl, :].ap.rearrange("p (h d) -> p h d", d=64),
                kfac[:, j, :].ap.unsqueeze(2).to_broadcast([128, NH, 64]), ALU.mult),
                reads=[vst[:, sl, :], kfac[:, j, :]], writes=[Vp[:, j, :, :]])
            add("pool", lambda e, j=j: e.tensor_copy(Vp[:, j, :, 64:65].ap, kfac[:, j, :].ap.unsqueeze(2)),
                reads=[kfac[:, j, :]], writes=[Vp[:, j, :, :]])
        run_sequence(2, xs, T, PT, (ys, ks, vs, lfs, convs), True)

    print("sbuf bytes remaining", nc.sbuf_bytes_remaining)
    sc.emit()
    stack.__exit__(None, None, None)
    return nc, sc


def make_in_map(inp, core, cfg):
    NSEQ = cfg["NSEQ"]
    f = lambda a: np.ascontiguousarray(np.asarray(a, dtype=np.float32))
    cvec = np.zeros((4, D), np.float32)
    cvec[0:NSEQ] = inp["c_prompt"][core * NSEQ:(core + 1) * NSEQ]
    cvec[2] = inp["c_sample"][core]
    m = {
        "xp": f(inp["x_prompt"][core * NSEQ:(core + 1) * NSEQ]),
        "cvec": cvec,
        "xs": f(inp["x_sample"][core]),
        "ck": f(inp["cache_k"][0, core]).reshape(-1, ATT),
        "cv": f(inp["cache_v"][0, core]).reshape(-1, ATT),
        "clf": f(inp["cache_logf"][0, core]),
        "cconv": f(inp["cache_conv"][0, core]),
        "w_ada": f(inp["w_ada"][0]), "b_ada": f(inp["b_ada"][0]),
        "g1": f(inp["g_norm1"][0]), "g2": f(inp["g_norm2"][0]),
        "w_in": f(inp["w_in"][0]), "b_f": f(inp["b_f"][0]), "w_conv": f(inp["w_conv"][0]),
        "g_att": f(inp["g_attn_out"][0]), "g_conv": f(inp["g_conv_out"][0]),
        "w_out": f(inp["w_out"][0]), "w_up": f(inp["w_up"][0]), "w_down": f(inp["w_down"][0]),
        "w_adaf": f(inp["w_ada_final"]), "b_adaf": f(inp["b_ada_final"]), "g_f": f(inp["g_final"]),
    }
    return m


_CFG = dict(S=4096, NSEQ=2, P=4096, T=16, SAMPLE=True)


def kernel(**inputs):
    cfg = dict(_CFG)
    inp = {k: np.asarray(v) for k, v in inputs.items()}
    nc, _ = build_program(cfg)
    in_maps = [make_in_map(inp, c, cfg) for c in range(NCORES)]
    res = run_bass_kernel_spmd(nc, in_maps, core_ids=list(range(NCORES)))
    r = res.results
    S, T = cfg["S"], cfg["T"]
    cat = lambda name: np.concatenate([np.asarray(r[c][name], dtype=np.float32) for c in range(NCORES)], axis=0)
    st = lambda name: np.stack([np.asarray(r[c][name], dtype=np.float32) for c in range(NCORES)], axis=0)
    y_prompt = cat("yp").reshape(16, S, D)
    y_sample = st("ys").reshape(8, T, D)
    kp = cat("kp").reshape(1, 16, S, NH, HD)
    vp = cat("vp").reshape(1, 16, S, NH, HD)
    lfp = cat("lfp").reshape(1, 16, S, NH)
    convp = cat("convp").reshape(1, 16, 2, CONV)
    ks = st("ks").reshape(1, 8, T, NH, HD)
    vs = st("vs").reshape(1, 8, T, NH, HD)
    lfs = st("lfs").reshape(1, 8, T, NH)
    convs = st("convs").reshape(1, 8, 2, CONV)
    return (y_prompt, y_sample, kp, vp, lfp, convp, ks, vs, lfs, convs)
```
